# Optimizing a Trainium2 kernel written in Bass

```python
import math
import jax, jax.numpy as jnp
from jax import lax
import numpy as np

D_MODEL = 1024
BATCH = 32
SEQ = 2048
DEPTH = 1

EPS = 1e-6
A_WIDTH = 768
A_GROUPS = 4
A_GROUP_DIM = A_WIDTH // A_GROUPS
CHUNK = 128
B_PATTERNS = ((128, 1), (512, 4), (2048, 16))
B_GROUPS = len(B_PATTERNS)
B_HEADS_PER_GROUP = 4
B_HEADS = B_GROUPS * B_HEADS_PER_GROUP
B_HEAD_DIM = 64
B_QKV_WIDTH = B_HEADS * B_HEAD_DIM
B_OUT_WIDTH = B_HEADS_PER_GROUP * B_HEAD_DIM
BLOCK = 128
MEM_LEN = 256
M_HEADS = 4
M_HEAD_DIM = 128
M_WIDTH = M_HEADS * M_HEAD_DIM
N_BRANCHES = 3
REL_BUCKETS = 32
REL_MAX_DISTANCE = 2048
IN_SIZES = (A_WIDTH, A_WIDTH, A_WIDTH,
            B_QKV_WIDTH, B_QKV_WIDTH, B_QKV_WIDTH, B_OUT_WIDTH,
            M_WIDTH, M_WIDTH,
            N_BRANCHES * D_MODEL)
IN_TOTAL = sum(IN_SIZES)

kernel_name = "hybrid_sgu_dilated_memory_block"


def rms_norm(x, w):
    xf = x.astype(jnp.float32)
    y = xf * lax.rsqrt(jnp.mean(xf * xf, axis=-1, keepdims=True) + EPS)
    return (y * w.astype(jnp.float32)).astype(x.dtype)


def layer_norm(x, w, b):
    xf = x.astype(jnp.float32)
    mu = jnp.mean(xf, axis=-1, keepdims=True)
    xc = xf - mu
    y = xc * lax.rsqrt(jnp.mean(xc * xc, axis=-1, keepdims=True) + EPS)
    return (y * w.astype(jnp.float32) + b.astype(jnp.float32)).astype(x.dtype)


def t5_causal_bucket(dist):
    max_exact = REL_BUCKETS // 2
    is_small = dist < max_exact
    df = jnp.maximum(dist, 1).astype(jnp.float32)
    large = max_exact + (jnp.log(df / max_exact) / math.log(REL_MAX_DISTANCE / max_exact)
                         * (REL_BUCKETS - max_exact)).astype(jnp.int32)
    large = jnp.minimum(large, REL_BUCKETS - 1)
    return jnp.where(is_small, dist, large)


def chunked_spatial_gating(u, v, w_s, b_s):
    bn, s, _ = u.shape
    nc = s // CHUNK
    causal = jnp.tril(jnp.ones((CHUNK, CHUNK), dtype=bool))
    w = jnp.where(causal[None], w_s, 0).astype(v.dtype)
    vc = v.reshape(bn, nc, CHUNK, A_GROUPS, A_GROUP_DIM)
    mixed = jnp.einsum('gts,bcsgd->bctgd', w, vc) + b_s.T.astype(v.dtype)[None, None, :, :, None]
    return u * mixed.reshape(bn, s, A_WIDTH)


def dilated_window_attention(q, k, v, bias_table, dilation, win_steps):
    bn, s, h, hd = q.shape
    L = s // dilation
    bd = bn * dilation

    def to_residue(t):
        return jnp.moveaxis(t.reshape((bn, L, dilation) + t.shape[2:]), 2, 1).reshape((bd, L) + t.shape[2:])

    def from_residue(t):
        t = t.reshape((bn, dilation, L) + t.shape[2:])
        return jnp.moveaxis(t, 1, 2).reshape((bn, s) + t.shape[3:])

    qs, ks, vs = to_residue(q), to_residue(k), to_residue(v)
    nb = -(-L // BLOCK)
    lp = nb * BLOCK
    pad = lp - L
    qb = jnp.pad(qs, ((0, 0), (0, pad), (0, 0), (0, 0))).reshape(bd, nb, BLOCK, h, hd)

    def kv_blocks(t):
        tp = jnp.pad(t, ((0, 0), (BLOCK, pad), (0, 0), (0, 0))).reshape(bd, nb + 1, BLOCK, h, hd)
        return jnp.concatenate([tp[:, :-1], tp[:, 1:]], axis=2)

    kb, vb = kv_blocks(ks), kv_blocks(vs)
    qi = jnp.arange(BLOCK)[:, None]
    kj = jnp.arange(2 * BLOCK)[None, :]
    step = qi + BLOCK - kj
    in_window = (step >= 0) & (step <= win_steps)
    key_pos = jnp.arange(nb)[:, None, None] * BLOCK - BLOCK + kj[None]
    valid = in_window[None] & (key_pos >= 0)
    bucket = t5_causal_bucket(jnp.maximum(step, 0) * dilation)
    bias = jnp.transpose(bias_table[bucket], (2, 0, 1)).astype(jnp.float32)

    scores = jnp.einsum('bnqhd,bnkhd->bnhqk', qb.astype(jnp.float32), kb.astype(jnp.float32)) * (hd ** -0.5)
    scores = jnp.where(valid[None, :, None], scores + bias[None, None], -1e30)
    m = jnp.max(scores, axis=-1, keepdims=True)
    p = jnp.exp(scores - m)
    l = jnp.sum(p, axis=-1)
    o = jnp.einsum('bnhqk,bnkhd->bnqhd', p, vb.astype(jnp.float32))
    o = o / jnp.transpose(l, (0, 1, 3, 2))[..., None]
    lse = jnp.transpose(m[..., 0] + jnp.log(l), (0, 1, 3, 2))
    o = o.reshape(bd, lp, h, hd)[:, :L]
    lse = lse.reshape(bd, lp, h)[:, :L]
    return from_residue(o), from_residue(lse)


def setup_inputs(seed: int = 0) -> dict:
    key = jax.random.key(seed)
    ks = jax.random.split(key, 22)
    f32 = jnp.float32
    n = lambda k, shape: jax.random.normal(k, shape, f32)
    return {
        "x": n(ks[0], (BATCH, SEQ, D_MODEL)),
        "mem": n(ks[1], (BATCH, MEM_LEN, D_MODEL)),
        "norm_w": 1.0 + 0.02 * n(ks[2], (D_MODEL,)),
        "w_in": n(ks[3], (D_MODEL, IN_TOTAL)) * D_MODEL ** -0.5,
        "gate_b": 0.01 * n(ks[4], (N_BRANCHES, D_MODEL)),
        "a_v_norm_w": 1.0 + 0.02 * n(ks[5], (A_WIDTH,)),
        "a_v_norm_b": 0.02 * n(ks[6], (A_WIDTH,)),
        "a_spatial_w": n(ks[7], (A_GROUPS, CHUNK, CHUNK)) * CHUNK ** -0.5,
        "a_spatial_b": 1.0 + 0.02 * n(ks[8], (A_GROUPS, CHUNK)),
        "b_q_norm_w": 1.0 + 0.02 * n(ks[9], (B_HEAD_DIM,)),
        "b_k_norm_w": 1.0 + 0.02 * n(ks[10], (B_HEAD_DIM,)),
        "rel_bias": 0.2 * n(ks[11], (REL_BUCKETS, B_HEADS)),
        "mem_norm_w": 1.0 + 0.02 * n(ks[12], (D_MODEL,)),
        "m_w_kv": n(ks[13], (D_MODEL, 2 * M_WIDTH)) * D_MODEL ** -0.5,
        "m_q_norm_w": 1.0 + 0.02 * n(ks[14], (M_HEAD_DIM,)),
        "m_k_norm_w": 1.0 + 0.02 * n(ks[15], (M_HEAD_DIM,)),
        "proj_a": n(ks[16], (A_WIDTH, D_MODEL)) * A_WIDTH ** -0.5,
        "proj_b": n(ks[17], (B_OUT_WIDTH, D_MODEL)) * B_OUT_WIDTH ** -0.5,
        "proj_m": n(ks[18], (M_WIDTH, D_MODEL)) * M_WIDTH ** -0.5,
        "w_out": n(ks[19], (D_MODEL, D_MODEL)) * D_MODEL ** -0.5,
    }


def reference(x, mem, norm_w, w_in, gate_b, a_v_norm_w, a_v_norm_b, a_spatial_w, a_spatial_b,
              b_q_norm_w, b_k_norm_w, rel_bias, mem_norm_w, m_w_kv, m_q_norm_w, m_k_norm_w,
              proj_a, proj_b, proj_m, w_out):
    bn, s, _ = x.shape
    split_at = np.cumsum(IN_SIZES)[:-1].tolist()
    for _layer in range(DEPTH):
        h = rms_norm(x, norm_w)
        proj = h @ w_in
        a_u, a_v, a_z, b_q, b_k, b_v, b_z, m_q, m_z, g = jnp.split(proj, split_at, axis=-1)

        u = jax.nn.gelu(a_u, approximate=False)
        vv = layer_norm(jax.nn.gelu(a_v, approximate=False), a_v_norm_w, a_v_norm_b)
        y_a = chunked_spatial_gating(u, vv, a_spatial_w, a_spatial_b) * jax.nn.silu(a_z)

        q = rms_norm(b_q.reshape(bn, s, B_HEADS, B_HEAD_DIM), b_q_norm_w)
        k = rms_norm(b_k.reshape(bn, s, B_HEADS, B_HEAD_DIM), b_k_norm_w)
        v = b_v.reshape(bn, s, B_HEADS, B_HEAD_DIM)
        outs, lses = [], []
        for gi, (window, dilation) in enumerate(B_PATTERNS):
            hs = slice(gi * B_HEADS_PER_GROUP, (gi + 1) * B_HEADS_PER_GROUP)
            o_g, lse_g = dilated_window_attention(q[:, :, hs], k[:, :, hs], v[:, :, hs],
                                                  rel_bias[:, hs], dilation, window // dilation)
            outs.append(o_g)
            lses.append(lse_g)
        o_all = jnp.stack(outs, axis=0)
        wts = jax.nn.softmax(jnp.stack(lses, axis=0), axis=0)
        y_b = jnp.sum(wts[..., None] * o_all, axis=0).reshape(bn, s, B_OUT_WIDTH).astype(x.dtype)
        y_b = y_b * jax.nn.silu(b_z)

        kv = rms_norm(mem, mem_norm_w) @ m_w_kv
        mk, mv = jnp.split(kv, 2, axis=-1)
        mq = rms_norm(m_q.reshape(bn, s, M_HEADS, M_HEAD_DIM), m_q_norm_w)
        mk = rms_norm(mk.reshape(bn, MEM_LEN, M_HEADS, M_HEAD_DIM), m_k_norm_w)
        mv = mv.reshape(bn, MEM_LEN, M_HEADS, M_HEAD_DIM)
        sc = jnp.einsum('bshd,bmhd->bhsm', mq.astype(jnp.float32), mk.astype(jnp.float32)) * (M_HEAD_DIM ** -0.5)
        pm = jax.nn.softmax(sc, axis=-1)
        y_m = jnp.einsum('bhsm,bmhd->bshd', pm, mv.astype(jnp.float32)).reshape(bn, s, M_WIDTH).astype(x.dtype)
        y_m = y_m * jax.nn.silu(m_z)

        gates = jax.nn.sigmoid((g.reshape(bn, s, N_BRANCHES, D_MODEL) + gate_b).astype(jnp.float32)).astype(x.dtype)
        merged = (gates[:, :, 0] * (y_a @ proj_a)
                  + gates[:, :, 1] * (y_b @ proj_b)
                  + gates[:, :, 2] * (y_m @ proj_m))
        x = x + merged @ w_out
    return x
```

```python
import contextlib
import os
import numpy as np
import concourse.bass as bass
import concourse.mybir as mybir
from concourse.bass_utils import run_bass_kernel_spmd

F32 = mybir.dt.float32
BF16 = mybir.dt.bfloat16
AF = mybir.ActivationFunctionType
ALU = mybir.AluOpType

NCORES = 8
BPC = 4
S = 2048
DM = 1024
EPS = 1e-6
NSLOT = 6
LOOKAHEAD = 2
SAME_SYNC = os.environ.get("KSAME", "1") == "1"

IMG = {}
_names = ["B0", "B1", "B2", "B3", "B4", "M0a", "M0b", "V0", "V1", "U0", "Z0", "U1", "Z1",
          "MQ", "MZ", "G0", "G1", "G2", "G3", "G4", "G5", "PA0", "PA1", "PB", "PM", "WO0", "WO1"]
for _i, _n in enumerate(_names):
    IMG[_n] = _i
NIMG = len(_names)

PC_NORMW = 0
PC_MEMW = 8
PC_GATEB = 16
PC_BQW = 40
PC_BKW = 41
PC_MQW = 42
PC_MKW = 43
PC_NEGHALF = 44
PC_BS = 48
PC_LNWC = 560
PC_LNBC = 568
NP = 576


def _bucket(dist):
    dist = dist.astype(np.int32)
    max_exact = 16
    is_small = dist < max_exact
    df = np.maximum(dist, 1).astype(np.float32)
    large = max_exact + (np.log(df / np.float32(max_exact)) / np.float32(np.log(2048 / max_exact))
                         * np.float32(32 - max_exact)).astype(np.int32)
    large = np.minimum(large, 31)
    return np.where(is_small, dist, large)


class Prog:
    ENGS = ["pe", "act", "dve", "pool", "sp"]

    def __init__(self, nc, stack):
        self.nc = nc
        self.stack = stack
        self.eng = {"pe": nc.tensor, "act": nc.scalar, "dve": nc.vector, "pool": nc.gpsimd, "sp": nc.sync}
        self.sem = {}
        for e in self.ENGS:
            self.sem[("eng", e)] = stack.enter_context(nc.semaphore("sem_" + e))
        self.cnt = {}
        self.pending = {e: [] for e in self.ENGS}
        self.known = {e: {} for e in self.ENGS}
        self.last_writer = {}
        self.readers = {}
        self.nblocks = 0

    def _sem(self, key):
        if key not in self.sem:
            self.sem[key] = self.stack.enter_context(self.nc.semaphore("sem_%s" % (key[1],)))
        return self.sem[key]

    def op(self, eng, fn, r=(), w=(), dma=None):
        deps = set()
        w = list(w)
        if dma is not None:
            w.append(("dmatag", dma))
        for x in r:
            if x in self.last_writer:
                deps.add(self.last_writer[x])
        for x in w:
            if x in self.last_writer:
                deps.add(self.last_writer[x])
            for t in self.readers.get(x, ()):
                deps.add(t)
        if dma is None:
            key = ("eng", eng)
            self.cnt[key] = self.cnt.get(key, 0) + 1
        else:
            key = ("dma", dma)
            self._sem(key)
            self.cnt[key] = self.cnt.get(key, 0) + 16
        tok = (key, self.cnt[key])
        waits = {}
        for (k, v) in deps:
            if k == ("eng", eng) and (eng == "pe" or not SAME_SYNC) and dma is None:
                continue
            waits[k] = max(waits.get(k, 0), v)
        known = self.known[eng]
        final = []
        for k, v in waits.items():
            if known.get(k, 0) < v:
                final.append((k, v))
                known[k] = v
        self.pending[eng].append((final, fn, tok, dma is not None))
        for x in w:
            self.last_writer[x] = tok
            self.readers[x] = []
        for x in r:
            self.readers.setdefault(x, []).append(tok)
        return tok

    def flush(self):
        nc = self.nc
        self.nblocks += 1
        with nc.Block() as block:
            for name in self.ENGS:
                lst = self.pending[name]
                if not lst:
                    continue

                def body(e, lst=lst):
                    for waits, fn, tok, isdma in lst:
                        for k, v in waits:
                            e.wait_ge(self.sem[k], v)
                        ins = fn(e)
                        ins.then_inc(self.sem[tok[0]], 16 if isdma else 1)

                {"pe": block.tensor, "act": block.scalar, "dve": block.vector,
                 "pool": block.gpsimd, "sp": block.sync}[name](body)
        self.pending = {e: [] for e in self.ENGS}


def build(debug=False, stop=99, bstop=99):
    split_b = os.environ.get('KSPLIT', '0') == '1'
    nc = bass.Bass("TRN2", target_bir_lowering=False)
    x_d = nc.dram_tensor("x", [BPC, S, DM], F32, kind="ExternalInput").ap()
    mem_d = nc.dram_tensor("mem", [BPC, 256, DM], F32, kind="ExternalInput").ap()
    wimg_d = nc.dram_tensor("wimg", [NIMG, 1024, 512], F32, kind="ExternalInput").ap()
    params_d = nc.dram_tensor("params", [128, NP], F32, kind="ExternalInput").ap()
    wst_d = nc.dram_tensor("wst", [128, 512], F32, kind="ExternalInput").ap()
    cst_d = nc.dram_tensor("cst", [128, 512], F32, kind="ExternalInput").ap()
    bias_d = nc.dram_tensor("biasT", [5, 128, 512], F32, kind="ExternalInput").ap()
    out_d = nc.dram_tensor("out", [BPC, S, DM], F32, kind="ExternalOutput").ap()
    wscr_d = nc.dram_tensor("wscr", [NIMG, 128, 4096], BF16, kind="Internal").ap()
    dbg = {}
    if debug:
        dbg["hT"] = nc.dram_tensor("dbg_hT", [128, 8 * 512], BF16, kind="ExternalOutput").ap()
        dbg["ybT"] = nc.dram_tensor("dbg_ybT", [64, 4 * 2048], BF16, kind="ExternalOutput").ap()
        dbg["yaT"] = nc.dram_tensor("dbg_yaT", [128, 8 * 512], BF16, kind="ExternalOutput").ap()
        dbg["ymT"] = nc.dram_tensor("dbg_ymT", [128, 4 * 512], BF16, kind="ExternalOutput").ap()
        dbg["mg"] = nc.dram_tensor("dbg_mg", [128, 8 * 512], BF16, kind="ExternalOutput").ap()

    with contextlib.ExitStack() as stack:
        P = Prog(nc, stack)
        op = P.op

        uniq = [0]

        def un(name):
            uniq[0] += 1
            return "s%d_%s" % (uniq[0], name)

        def sb(name, shape, dt):
            return stack.enter_context(nc.sbuf_tensor(un(name), shape, dt))

        params = sb("params", [128, NP], F32)
        ident = sb("ident", [128, 128], BF16)
        blk64 = sb("blk64", [128, 128], BF16)
        ones_bf = sb("ones_bf", [128, 128], BF16)
        sel64 = sb("sel64", [128, 128], BF16)
        wsT = sb("wsT", [128, 4, 128], BF16)
        Cc = sb("Cc", [128, 8, 128], F32)
        E = sb("E", [128, 5, 512], BF16)
        ring = sb("ring", [128, NSLOT, 8, 512], BF16)
        xb = sb("xb", [128, 2, 1024], F32)
        xn = sb("xn", [128, 4, 1024], BF16)
        ybT = sb("ybT", [128, 4, 2048], BF16)
        mkT = sb("mkT", [128, 4, 256], BF16)
        mv = sb("mv", [128, 2, 512], BF16)
        stat = sb("stat", [128, 64], F32)
        ps = stack.enter_context(nc.psum_tensor("psum_all", [128, 8, 512], F32))

        bankctr = [0]

        def bank():
            i = bankctr[0] % 8
            bankctr[0] += 1
            return i

        def PS(i):
            return ps[:, i, :]

        uses = []
        for b in range(BPC):
            uses += ["M0a", "M0b", "B0", "B1", "B2", "B3", "B4"]
            for st in range(4):
                uses += ["V0", "V1", "U0", "Z0", "U1", "Z1", "G0", "G1", "PA0", "PA1",
                         "MQ", "MZ", "G2", "G3", "PB", "G4", "G5", "PM", "WO0", "WO1"]
        wstate = {"issued": 0, "ptr": 0, "cached": set()}

        def w_issue(n):
            name = uses[n]
            slot = n % NSLOT
            img = IMG[name]
            dst = ring[:, slot, :, :]
            dflat = dst.rearrange("p k c -> p (k c)")
            if name in wstate["cached"]:
                op("sp", lambda e, dflat=dflat, img=img: e.dma_start(out=dflat, in_=wscr_d[img]),
                   r=[("wscr", img)], w=[("ring", slot)], dma="ringH%d" % slot)
            else:
                src = wimg_d[img].rearrange("(k p) c -> p k c", p=128)
                op("pool", lambda e, dst=dst, src=src: e.dma_start(out=dst, in_=src),
                   w=[("ring", slot)], dma="ring%d" % slot)
                op("sp", lambda e, dflat=dflat, img=img: e.dma_start(out=wscr_d[img], in_=dflat),
                   r=[("ring", slot)], w=[("wscr", img)], dma="wst%d" % slot)
                wstate["cached"].add(name)

        def wget(name):
            n = wstate["ptr"]
            assert uses[n] == name, (uses[n], name, n)
            wstate["ptr"] += 1
            while wstate["issued"] < min(len(uses), n + 1 + LOOKAHEAD):
                w_issue(wstate["issued"])
                wstate["issued"] += 1
            slot = n % NSLOT
            return ring[:, slot, :, :], ("ring", slot)

        op("sp", lambda e: e.dma_start(out=params[:], in_=params_d), w=["params"], dma="params")
        op("pool", lambda e: e.memset(ones_bf[:], 1.0), w=["ones_bf"])
        op("pool", lambda e: e.memset(ybT[:], 0.0), w=[("ybT", i_) for i_ in range(4)])
        with contextlib.ExitStack() as phs:
            cstf = phs.enter_context(nc.sbuf_tensor(un("cstf"), [128, 512], F32))
            wstf = phs.enter_context(nc.sbuf_tensor(un("wstf"), [128, 512], F32))
            bt = phs.enter_context(nc.sbuf_tensor(un("bt"), [128, 512], F32))
            bt2 = phs.enter_context(nc.sbuf_tensor(un("bt2"), [128, 512], F32))
            op("sp", lambda e: e.dma_start(out=cstf[:], in_=cst_d), w=["cstf"], dma="cstf")
            op("sp", lambda e: e.dma_start(out=wstf[:], in_=wst_d), w=["wstf"], dma="wstf")
            op("dve", lambda e: e.tensor_copy(out=ident[:], in_=cstf[:, 0:128]), r=["cstf"], w=["ident"])
            op("dve", lambda e: e.tensor_copy(out=blk64[:], in_=cstf[:, 128:256]), r=["cstf"], w=["blk64"])
            op("dve", lambda e: e.tensor_copy(out=sel64[:], in_=cstf[:, 64:65].broadcast_to([128, 128])), r=["cstf"], w=["sel64"])
            op("dve", lambda e: e.tensor_tensor(
                out=wsT[:], in0=wstf[:].rearrange("p (g t) -> p g t", g=4),
                in1=cstf[:, 384:512].unsqueeze(1).broadcast_to([128, 4, 128]), op=ALU.mult),
               r=["wstf", "cstf"], w=["wsT"])
            brs = bank()
            op("pe", lambda e: e.matmul(PS(brs), lhsT=ones_bf[:], rhs=wsT[:].rearrange("p g t -> p (g t)"), start=True, stop=True),
               r=["ones_bf", "wsT"], w=[("ps", brs)])
            for jc in range(8):
                g_ = jc // 2
                op("dve", lambda e, jc=jc, g_=g_: e.scalar_tensor_tensor(
                    out=Cc[:, jc, :], in0=PS(brs)[:, g_ * 128:(g_ + 1) * 128], scalar=params[:, PC_LNBC + jc:PC_LNBC + jc + 1],
                    in1=params[:, PC_BS + 128 * g_:PC_BS + 128 * g_ + 128], op0=ALU.mult, op1=ALU.add),
                   r=[("ps", brs), "params"], w=["Cc"])
            for idx in range(5):
                mcol = 256 if idx in (0, 2) else 384
                op("sp", lambda e, idx=idx: e.dma_start(out=bt[:], in_=bias_d[idx]), w=["bt"], dma="bt")
                op("act", lambda e: e.activation(out=bt2[:], in_=bt[:], func=AF.Exp), r=["bt"], w=["bt2"])
                op("dve", lambda e, idx=idx, mcol=mcol: e.tensor_tensor(
                    out=E[:, idx, :].rearrange("p (h q) -> p h q", h=4),
                    in0=bt2[:].rearrange("p (h q) -> p h q", h=4),
                    in1=cstf[:, mcol:mcol + 128].unsqueeze(1).broadcast_to([128, 4, 128]), op=ALU.mult),
                   r=["bt2", "cstf"], w=[("E", idx)])
            P.flush()

        xctr = [0]

        def norm_a(src_dram, act_rstd=False):
            j = xctr[0] % 2
            sl = xctr[0] % 4
            xctr[0] += 1
            xt = xb[:, j, :]
            xnt = xn[:, sl, :]
            ssq = stat[:, 2 * sl:2 * sl + 1]
            rstd = stat[:, 2 * sl + 1:2 * sl + 2]
            op("sp", lambda e: e.dma_start(out=xt, in_=src_dram), w=[("xb", j)], dma="xb%d" % j)
            op("act", lambda e: e.activation(out=xnt, in_=xt, func=AF.Square, accum_out=ssq),
               r=[("xb", j)], w=[("xn", sl), ("ssq", sl)])
            if act_rstd:
                op("act", lambda e: e.activation(out=rstd, in_=ssq, func=AF.Ln, scale=1.0 / DM, bias=EPS),
                   r=[("ssq", sl)], w=[("rstd", sl)])
                op("act", lambda e: e.activation(out=rstd, in_=rstd, func=AF.Exp, scale=-0.5),
                   r=[("rstd", sl)], w=[("rstd", sl)])
            else:
                op("dve", lambda e: e.tensor_scalar(out=rstd, in0=ssq, scalar1=1.0 / DM, scalar2=EPS,
                                                    op0=ALU.mult, op1=ALU.add), r=[("ssq", sl)], w=[("rstd", sl)])
                op("pool", lambda e: e.tensor_tensor(out=rstd, in0=rstd, in1=params[:, PC_NEGHALF:PC_NEGHALF + 1],
                                                     op=ALU.pow), r=[("rstd", sl), "params"], w=[("rstd", sl)])
            op("dve", lambda e: e.tensor_scalar(out=xnt, in0=xt, scalar1=rstd, scalar2=None, op0=ALU.mult),
               r=[("xb", j), ("rstd", sl)], w=[("xn", sl)])
            return sl

        def norm_b(sl, wcol0, dst3, dst_res):
            xnt = xn[:, sl, :]
            bi = bank()
            pT = PS(bi).bitcast(BF16).rearrange("p (k t) -> p k t", k=8)
            for k in range(8):
                op("pe", lambda e, k=k: e.transpose(out=pT[:, k, :], in_=xnt[:, k * 128:(k + 1) * 128], identity=ident[:]),
                   r=[("xn", sl), "ident"], w=[("ps", bi)])
            op("dve", lambda e: e.tensor_tensor(
                out=dst3, in0=pT, in1=params[:, wcol0:wcol0 + 8].unsqueeze(2).broadcast_to([128, 8, 128]),
                op=ALU.mult), r=[("ps", bi), "params"], w=dst_res)

        def norm_transpose(src_dram, wcol0, dst3, dst_res):
            norm_b(norm_a(src_dram), wcol0, dst3, dst_res)

        def mm_acc(out_ap, pairs, bi, reads):
            n = len(pairs)
            for i, (l, rr) in enumerate(pairs):
                op("pe", lambda e, l=l, rr=rr, i=i: e.matmul(out_ap, lhsT=l, rhs=rr, start=(i == 0), stop=(i == n - 1)),
                   r=reads, w=[("ps", bi)])

        def feat_norm(bi, width, ones_l, sqrt_scale, sqrt_bias, wcol, out_ap, out_res, tmp_sq, tmp_sr, tmp_res, split=False):
            src = PS(bi)[:, 0:width]
            op("act", lambda e: e.activation(out=tmp_sq[:, 0:width], in_=src, func=AF.Square),
               r=[("ps", bi)], w=[tmp_res + "sq"])

            def tail():
                _feat_norm_tail(bi, width, ones_l, sqrt_scale, sqrt_bias, wcol, out_ap, out_res, tmp_sq, tmp_sr, tmp_res, src)
            if split:
                return tail
            tail()

        def _feat_norm_tail(bi, width, ones_l, sqrt_scale, sqrt_bias, wcol, out_ap, out_res, tmp_sq, tmp_sr, tmp_res, src):
            b2 = bank()
            op("pe", lambda e: e.matmul(PS(b2)[:, 0:width], lhsT=ones_l, rhs=tmp_sq[:, 0:width], start=True, stop=True),
               r=[tmp_res + "sq", "ones_bf", "blk64"], w=[("ps", b2)])
            op("act", lambda e: e.activation(out=tmp_sr[:, 0:width], in_=PS(b2)[:, 0:width], func=AF.Ln,
                                             scale=sqrt_scale, bias=sqrt_bias),
               r=[("ps", b2), "params"], w=[tmp_res + "sr"])
            op("act", lambda e: e.activation(out=tmp_sr[:, 0:width], in_=tmp_sr[:, 0:width], func=AF.Exp, scale=-0.5),
               r=[tmp_res + "sr"], w=[tmp_res + "sr"])
            op("dve", lambda e: e.scalar_tensor_tensor(out=out_ap, in0=src, scalar=params[:, wcol:wcol + 1],
                                                       in1=tmp_sr[:, 0:width], op0=ALU.mult, op1=ALU.mult),
               r=[("ps", bi), tmp_res + "sr", "params"], w=out_res)

        def dbg_store(name, src_ap, src_res, ncols, npart=128):
            if not debug:
                return
            op("sp", lambda e: e.dma_start(out=dbg[name], in_=src_ap), r=src_res, w=[], dma="dbg")


        for b in range(BPC):
            if stop <= 2 * b:
                break
            with contextlib.ExitStack() as phs:
                def pt(name, shape, dt):
                    return phs.enter_context(nc.sbuf_tensor(un(name), shape, dt))
                hT = pt("hT", [128, 8, 2048], BF16)
                qg = pt("qg", [128, 2, 2048], BF16)
                kg = pt("kg", [128, 2, 2048], BF16)
                vg = pt("vg", [128, 16, 4, 65], BF16)
                acc = pt("acc", [65, 4, 2048], F32)
                tsq = pt("tsq", [128, 2, 512], BF16)
                tsr = pt("tsr", [128, 2, 512], F32)
                ex = pt("ex", [128, 2, 512], BF16)
                pTt = pt("pT", [128, 4, 512], BF16)
                bszraw = pt("bszraw", [128, 1024], F32)
                memhT = bszraw[:].bitcast(BF16).rearrange("p (k m) -> p k m", k=8)
                bsz = bszraw[0:64, :].rearrange("p (a c) -> p a c", a=2)
                bt1 = tsr
                qm = pt("qm", [128, 4, 4, 128], BF16)
                op("pool", lambda e: e.memset(qm[:], 0.0), w=[("qm", i_) for i_ in range(4)])
                tctr = [0]
                pctr = [0]
                ptail = [None]

                def tmpi():
                    tctr[0] += 1
                    return tctr[0] % 2

                for mt in range(2):
                    norm_transpose(mem_d[b, mt * 128:(mt + 1) * 128, :], PC_MEMW,
                                   memhT[:, :, mt * 128:(mt + 1) * 128], [("memhT", mt)])
                wk, wkr = wget("M0a")
                for h in range(4):
                    bi = bank()
                    mm_acc(PS(bi)[:, 0:256], [(wk[:, k, h * 128:(h + 1) * 128], memhT[:, k, :]) for k in range(8)],
                           bi, [wkr, ("memhT", 0), ("memhT", 1)])
                    ti = tmpi()
                    feat_norm(bi, 256, ones_bf[:], 1.0 / 128, EPS, PC_MKW, mkT[:, h, :], [("mkT", h)],
                              tsq[:, ti, :], tsr[:, ti, :], "t%d" % ti)
                wv, wvr = wget("M0b")
                for mt in range(2):
                    bi = bank()
                    mm_acc(PS(bi), [(memhT[:, k, mt * 128:(mt + 1) * 128], wv[:, k, :]) for k in range(8)],
                           bi, [wvr, ("memhT", mt)])
                    op("act", lambda e, bi=bi, mt=mt: e.activation(out=mv[:, mt, :], in_=PS(bi), func=AF.Copy),
                       r=[("ps", bi)], w=[("mv", mt)])

                if bstop <= 1:
                    P.flush()
                    break
                hsl = {}
                for t in range(3):
                    hsl[t] = norm_a(x_d[b, t * 128:(t + 1) * 128, :], act_rstd=True)
                for t in range(16):
                    if t + 3 < 16:
                        hsl[t + 3] = norm_a(x_d[b, (t + 3) * 128:(t + 4) * 128, :], act_rstd=True)
                    norm_b(hsl[t], PC_NORMW, hT[:, :, t * 128:(t + 1) * 128], [("hT", t // 4, t % 4)])
                hT_all = [("hT", a, c) for a in range(4) for c in range(4)]
                op("pool", lambda e: e.memset(vg[:, :, :, 64:65], 1.0), w=["vg_ones"])

                def tok_view(ap3, g, blk512):
                    if g == 0:
                        return ap3[:, blk512 * 512:(blk512 + 1) * 512]
                    if g == 1:
                        return ap3[:, blk512:2048:4]
                    return ap3.rearrange("p (m r) -> p r m", r=16)[:, 4 * blk512:4 * blk512 + 4, :]

                def tok_view128(ap3, g, qb):
                    d = (1, 4, 16)[g]
                    nb = 16 // d
                    r, n = qb // nb, qb % nb
                    s0 = r + d * 128 * n
                    return ap3[:, s0:s0 + d * 127 + 1:d] if d > 1 else ap3[:, s0:s0 + 128]

                if bstop <= 2:
                    P.flush()
                    break
                if split_b:
                    P.flush()
                hblk = [xb[:].rearrange("p a c -> p (a c)").bitcast(BF16).rearrange("p (k t) -> p k t", k=8),
                        xn[:].rearrange("p a c -> p (a c)").rearrange("p (k t) -> p k t", k=8)]
                hbres = [[("xb", 0), ("xb", 1)], [("xn", 0), ("xn", 1), ("xn", 2), ("xn", 3)]]
                slotB = {}
                for g in range(3):
                    if bstop <= 3 + 2 * g:
                        break
                    d = (1, 4, 16)[g]
                    nb = 16 // d
                    wqk, wqkr = wget(("B0", "B2", "B3")[g])
                    slotB[g] = (wqk, wqkr)
                    if g == 0:
                        wv01, wv01r = wget("B1")
                    if g == 2:
                        wv2, wv2r = wget("B4")
                    if g < 2:
                        wvv, wvvr, vcol = wv01, wv01r, 256 * g
                    else:
                        wvv, wvvr, vcol = wv2, wv2r, 256
                    for blk in range(4):
                        if g > 0:
                            hb = hblk[blk % 2]
                            hbr = hbres[blk % 2]
                            if g == 1:
                                src_v = hT[:, :, blk:2048:4]
                                dst_v = hb
                            else:
                                src_v = hT[:].rearrange("p k (m r) -> p k r m", r=16)[:, :, 4 * blk:4 * blk + 4, :]
                                dst_v = hb.rearrange("p k (r m) -> p k r m", r=4)
                            hbr_a, hbr_b = hbr[:len(hbr) // 2], hbr[len(hbr) // 2:]
                            op("pool", lambda e, src_v=src_v, dst_v=dst_v: e.tensor_copy(out=dst_v[:, 0:4], in_=src_v[:, 0:4]),
                               r=hT_all, w=hbr_a)
                            op("dve", lambda e, src_v=src_v, dst_v=dst_v: e.tensor_copy(out=dst_v[:, 4:8], in_=src_v[:, 4:8]),
                               r=hT_all, w=hbr_b)
                        for qk in range(2):
                            for pr in range(2):
                                bi = bank()
                                c0 = qk * 256 + pr * 128
                                hdeps = [("hT", blk, c_) for c_ in range(4)] if g == 0 else hbr
                                mm_acc(PS(bi), [(wqk[:, k, c0:c0 + 128], (tok_view(hT[:, k, :], g, blk) if g == 0 else hb[:, k, :]))
                                                for k in range(8)], bi, [wqkr] + hdeps)
                                ti = tmpi()
                                dst = (qg, kg)[qk]
                                if qk == 0:
                                    sc, bs_, wc = 1.0, 64 * EPS, PC_BQW
                                else:
                                    sc, bs_, wc = 1.0 / 64, EPS, PC_BKW
                                if ptail[0] is not None:
                                    ptail[0]()
                                ptail[0] = feat_norm(bi, 512, blk64[:], sc, bs_, wc, dst[:, pr, blk * 512:(blk + 1) * 512],
                                                     [(("qg", "kg")[qk], pr, blk)], tsq[:, ti, :], tsr[:, ti, :], "t%d" % ti,
                                                     split=True)
                        for half in range(2):
                            bi = bank()
                            for q2 in range(2):
                                qb = blk * 4 + half * 2 + q2
                                mm_acc(PS(bi)[:, q2 * 256:(q2 + 1) * 256],
                                       [((tok_view128(hT[:, k, :], g, qb) if g == 0 else hb[:, k, (qb % 4) * 128:(qb % 4 + 1) * 128]),
                                         wvv[:, k, vcol:vcol + 256]) for k in range(8)],
                                       bi, [wvvr] + ([("hT", qb // 4, qb % 4)] if g == 0 else hbr))
                                if ptail[0] is not None:
                                    ptail[0]()
                                    ptail[0] = None
                            qb0 = blk * 4 + half * 2
                            eng = "act" if half == 0 else "dve"
                            src = PS(bi).rearrange("p (q h c) -> p q h c", q=2, h=4)
                            dstv = vg[:, qb0:qb0 + 2, :, 0:64]
                            if eng == "act":
                                op("act", lambda e, src=src, dstv=dstv: e.activation(out=dstv, in_=src, func=AF.Copy),
                                   r=[("ps", bi)], w=[("vg", qb0), ("vg", qb0 + 1)])
                            else:
                                op("dve", lambda e, src=src, dstv=dstv: e.tensor_copy(out=dstv, in_=src),
                                   r=[("ps", bi)], w=[("vg", qb0), ("vg", qb0 + 1)])
                    if bstop <= 4 + 2 * g:
                        break
                    eidx = {0: (0, 1), 1: (2, 3), 2: (None, 4)}[g]
                    def q_copy(qc):
                        qs_ = qc % 4
                        op("dve", lambda e, qs_=qs_, qc=qc: e.tensor_copy(
                            out=qm[0:64, qs_, 0:4:2, :], in_=qg[0:64, :, qc * 128:(qc + 1) * 128]),
                           r=[("qg", 0, qc // 4), ("qg", 1, qc // 4)], w=[("qm", qs_)])
                        op("act", lambda e, qs_=qs_, qc=qc: e.activation(
                            out=qm[64:128, qs_, 1:4:2, :], in_=qg[64:128, :, qc * 128:(qc + 1) * 128], func=AF.Copy),
                           r=[("qg", 0, qc // 4), ("qg", 1, qc // 4)], w=[("qm", qs_)])

                    def att_front(qb):
                        r_, n_ = qb // nb, qb % nb
                        blks = []
                        if n_ > 0:
                            blks.append((qb - 1, eidx[0]))
                        blks.append((qb, eidx[1]))
                        pts = []
                        qs = qb % 4
                        for qc in ([0, 1, 2] if qb == 0 else ([qb + 2] if qb + 2 < 16 else [])):
                            q_copy(qc)
                        for (kb, ei) in blks:
                            bi = bank()
                            for h in range(4):
                                pr = h // 2
                                op("pe", lambda e, bi=bi, h=h, pr=pr, kb=kb, qs=qs: e.matmul(
                                    PS(bi)[:, h * 128:(h + 1) * 128],
                                    lhsT=kg[:, pr, kb * 128:(kb + 1) * 128],
                                    rhs=qm[:, qs, h, :], start=True, stop=True),
                                   r=[("kg", pr, kb // 4), ("qm", qs)], w=[("ps", bi)])
                            ti = tmpi()
                            op("act", lambda e, bi=bi, ti=ti: e.activation(out=ex[:, ti, :], in_=PS(bi), func=AF.Exp),
                               r=[("ps", bi)], w=[("ex", ti)])
                            pctr[0] += 1
                            pi = pctr[0] % 4
                            op("dve",
                               lambda e, ti=ti, pi=pi, ei=ei: e.tensor_tensor(out=pTt[:, pi, :], in0=ex[:, ti, :],
                                                                             in1=E[:, ei, :], op=ALU.mult),
                               r=[("ex", ti), ("E", ei)], w=[("pT", pi)])
                            pts.append((kb, pi))
                        return pts

                    def att_back(qb, pts):
                        r_, n_ = qb // nb, qb % nb
                        bi = bank()
                        for h in range(4):
                            for i, (kb, pi) in enumerate(pts):
                                op("pe", lambda e, bi=bi, h=h, kb=kb, pi=pi, i=i, n=len(pts): e.matmul(
                                    PS(bi)[0:65, h * 128:(h + 1) * 128], lhsT=vg[:, kb, h, :],
                                    rhs=pTt[:, pi, h * 128:(h + 1) * 128], start=(i == 0), stop=(i == n - 1)),
                                   r=[("vg", kb), "vg_ones", ("pT", pi)], w=[("ps", bi)])
                        s0 = r_ + d * 128 * n_
                        accv = acc[0:65, :, s0:s0 + d * 127 + 1:d] if d > 1 else acc[0:65, :, s0:s0 + 128]
                        pvv = PS(bi)[0:65, :].rearrange("p (h q) -> p h q", h=4)
                        if g == 0:
                            op("dve", lambda e, accv=accv, pvv=pvv: e.tensor_copy(out=accv, in_=pvv),
                               r=[("ps", bi)], w=["acc"])
                        else:
                            op("dve", lambda e, accv=accv, pvv=pvv: e.tensor_tensor(out=accv, in0=accv, in1=pvv, op=ALU.add),
                               r=[("ps", bi), "acc"], w=["acc"])

                    pend = None
                    for qb in range(16):
                        pts_ = att_front(qb)
                        if pend is not None:
                            att_back(*pend)
                        pend = (qb, pts_)
                    att_back(*pend)
                    if split_b and g < 2:
                        P.flush()
                if bstop <= 9:
                    P.flush()
                    break
                if split_b:
                    P.flush()
                bhl = pTt
                for blk in range(4):
                    for h in range(4):
                        bi = bank()
                        mm_acc(PS(bi), [(wv2[:, k, 64 * h:64 * h + 128], hT[:, k, blk * 512:(blk + 1) * 512])
                                        for k in range(8)], bi, [wv2r] + [("hT", blk, c_) for c_ in range(4)])
                        op("act", lambda e, bi=bi, h=h, blk=blk: e.activation(out=ybT[0:64, h, blk * 512:(blk + 1) * 512],
                                                                             in_=PS(bi)[0:64, :], func=AF.Silu),
                           r=[("ps", bi)], w=[("ybT", blk)])
                def fin_front(it, blk, h):
                    s_ = it % 2
                    asl = acc[0:65, h, blk * 512:(blk + 1) * 512]
                    op("act", lambda e, asl=asl, s_=s_: e.activation(out=bhl[0:65, 2 * s_, :], in_=asl, func=AF.Copy),
                       r=["acc"], w=[("pT", 2 * s_)])
                    op("dve", lambda e, asl=asl, s_=s_: e.tensor_tensor(out=bhl[0:65, 2 * s_ + 1, :], in0=asl, in1=bhl[0:65, 2 * s_, :],
                                                                      op=ALU.subtract),
                       r=["acc", ("pT", 2 * s_)], w=[("pT", 2 * s_ + 1)])
                    b2 = bank()
                    for i_ in range(2):
                        op("pe", lambda e, b2=b2, i_=i_, s_=s_: e.matmul(PS(b2), lhsT=sel64[:, :], rhs=bhl[:, 2 * s_ + i_, :],
                                                                        start=(i_ == 0), stop=(i_ == 1)),
                           r=[("pT", 2 * s_), ("pT", 2 * s_ + 1), "sel64"], w=[("ps", b2)])
                    return b2

                def fin_back(it, blk, h, b2):
                    ti = it % 2
                    op("act", lambda e, b2=b2, ti=ti: e.activation(out=bt1[0:64, ti, :], in_=PS(b2)[0:64, :], func=AF.Ln),
                       r=[("ps", b2)], w=["t%dsr" % ti])
                    op("act", lambda e, ti=ti: e.activation(out=bt1[0:64, ti, :], in_=bt1[0:64, ti, :], func=AF.Exp, scale=-1.0),
                       r=["t%dsr" % ti], w=["t%dsr" % ti])
                    op("dve", lambda e, ti=ti, h=h, blk=blk: e.tensor_tensor(
                        out=bt1[0:64, ti, :], in0=acc[0:64, h, blk * 512:(blk + 1) * 512], in1=bt1[0:64, ti, :], op=ALU.mult),
                       r=["acc", "t%dsr" % ti], w=["t%dsr" % ti])
                    op("pool", lambda e, ti=ti, h=h, blk=blk: e.tensor_tensor(
                        out=ybT[0:64, h, blk * 512:(blk + 1) * 512], in0=bt1[0:64, ti, :],
                        in1=ybT[0:64, h, blk * 512:(blk + 1) * 512], op=ALU.mult),
                       r=["t%dsr" % ti, ("ybT", blk)], w=[("ybT", blk)])

                fpend = None
                for it, (blk, h) in enumerate([(blk_, h_) for blk_ in range(4) for h_ in range(4)]):
                    b2_ = fin_front(it, blk, h)
                    if fpend is not None:
                        fin_back(*fpend)
                    fpend = (it, blk, h, b2_)
                fin_back(*fpend)
                if debug and b == 0:
                    dbg_store("ybT", ybT[0:64].rearrange("p h t -> p (h t)"), [("ybT", i) for i in range(4)], 8192, 64)
                P.flush()

            if stop <= 2 * b + 1:
                break
            with contextlib.ExitStack() as phs:
                def pt(name, shape, dt):
                    return phs.enter_context(nc.sbuf_tensor(un(name), shape, dt))
                hTs2 = pt("hTs", [128, 2, 8, 512], BF16)
                vv = pt("vv", [128, 4, 832], BF16)
                op("pool", lambda e: e.memset(vv[:], 0.0), w=[("vv", j_) for j_ in range(4)])
                gv = pt("gv", [128, 768], F32)
                bnst = pt("bnst", [128, 2, 6], F32)
                gu = pt("gu", [128, 2, 512], BF16)
                szt = pt("sz", [128, 2, 512], BF16)
                uz = pt("uz", [128, 2, 512], BF16)
                tb = pt("tb", [128, 2, 512], F32)
                yaT = pt("yaT", [128, 8, 512], BF16)
                op("pool", lambda e: e.memset(yaT[:], 0.0), w=[("yaT", j_) for j_ in range(8)])
                tsq2 = pt("tsq2", [128, 2, 512], BF16)
                tsr2 = pt("tsr2", [128, 2, 512], F32)
                mqn = pt("mqn", [128, 4, 512], BF16)
                szm = pt("szm", [128, 4, 512], BF16)
                pmT = pt("pm", [128, 4, 512], BF16)
                rl = pt("rl", [128, 2, 512], F32)
                ymT = pt("ymT", [128, 4, 512], BF16)
                sg = pt("sg", [128, 2, 512], F32)
                mg = pt("mg", [128, 8, 512], F32)
                mgb = pt("mgb", [128, 8, 512], BF16)
                res = pt("res", [128, 2, 1024], F32)
                tc2 = [0]

                def t2():
                    tc2[0] += 1
                    return tc2[0] % 2

                hslots = {}

                def emit_hT_a(st_):
                    hslots[st_] = [norm_a(x_d[b, st_ * 512 + j_ * 128:st_ * 512 + (j_ + 1) * 128, :]) for j_ in range(4)]

                def emit_hT_b(st_):
                    for j_ in range(4):
                        norm_b(hslots[st_][j_], PC_NORMW, hTs2[:, st_ % 2, :, j_ * 128:(j_ + 1) * 128], [("hTs", st_ % 2, j_)])

                emit_hT_a(0)
                emit_hT_b(0)
                for st in range(4):
                    T0 = st * 512
                    hTs = hTs2[:, st % 2]
                    hTs_all = [("hTs", st % 2, j) for j in range(4)]
                    if debug and b == 0 and st == 0:
                        dbg_store("hT", hTs.rearrange("p k t -> p (k t)"), hTs_all, 4096)
                    wV0, wV0r = wget("V0")
                    wV1, wV1r = wget("V1")
                    for j in range(4):
                        for half in range(2):
                            wv_, wvr_ = (wV0, wV0r) if half == 0 else (wV1, wV1r)
                            bi = bank()
                            mm_acc(PS(bi)[:, 0:384], [(hTs[:, k, j * 128:(j + 1) * 128], wv_[:, k, 0:384]) for k in range(8)],
                                   bi, [wvr_, ("hTs", st % 2, j)])
                            op("act", lambda e, bi=bi, half=half: e.activation(out=gv[:, half * 384:(half + 1) * 384],
                                                                               in_=PS(bi)[:, 0:384], func=AF.Gelu),
                               r=[("ps", bi)], w=[("gv", half)])
                            op("dve", lambda e, half=half: e.bn_stats(out=bnst[:, half, :], in_=gv[:, half * 384:(half + 1) * 384]),
                               r=[("gv", half)], w=[("bnst", half)])
                        mvv = stat[:, 8:10]
                        rs = stat[:, 10:11]
                        op("dve", lambda e: e.bn_aggr(out=mvv, in_=bnst[:].rearrange("p a b -> p (a b)")),
                           r=[("bnst", 0), ("bnst", 1)], w=["mvv"])
                        op("dve", lambda e: e.tensor_scalar(out=rs, in0=stat[:, 9:10], scalar1=EPS, scalar2=None, op0=ALU.add),
                           r=["mvv"], w=["rs"])
                        op("pool", lambda e: e.tensor_tensor(out=rs, in0=rs, in1=params[:, PC_NEGHALF:PC_NEGHALF + 1], op=ALU.pow),
                           r=["rs", "params"], w=["rs"])
                        op("dve", lambda e, j=j: e.tensor_scalar(out=vv[:, j, 0:768], in0=gv[:], scalar1=stat[:, 8:9], scalar2=rs,
                                                            op0=ALU.subtract, op1=ALU.mult),
                           r=[("gv", 0), ("gv", 1), "mvv", "rs"], w=[("vv", j)])
                    wU = {}
                    for jc in range(8):
                        g_, part = jc // 2, jc % 2
                        cw = 128 if part == 0 else 64
                        co = 192 * g_ + 128 * part
                        if jc == 0:
                            wU[0] = wget("U0")
                            wU[1] = wget("Z0")
                        if jc == 5:
                            wU[0] = wget("U1")
                            wU[1] = wget("Z1")
                        c0 = co if co < 512 else co - 512
                        (wu, wur), (wz, wzr) = wU[0], wU[1]
                        bu = bank()
                        mm_acc(PS(bu), [(wu[:, k, c0:c0 + 128], hTs[:, k, :]) for k in range(8)], bu, [wur] + hTs_all)
                        bz = bank()
                        mm_acc(PS(bz), [(wz[:, k, c0:c0 + 128], hTs[:, k, :]) for k in range(8)], bz, [wzr] + hTs_all)
                        ti = t2()
                        op("act", lambda e, bu=bu, ti=ti, cw=cw: e.activation(out=gu[0:cw, ti, :], in_=PS(bu)[0:cw, :], func=AF.Gelu),
                           r=[("ps", bu)], w=[("gu", ti)])
                        op("act", lambda e, bz=bz, ti=ti, cw=cw: e.activation(out=szt[0:cw, ti, :], in_=PS(bz)[0:cw, :], func=AF.Silu),
                           r=[("ps", bz)], w=[("sz", ti)])
                        op("pool", lambda e, ti=ti, cw=cw: e.tensor_tensor(out=uz[0:cw, ti, :], in0=gu[0:cw, ti, :],
                                                                            in1=szt[0:cw, ti, :], op=ALU.mult),
                           r=[("gu", ti), ("sz", ti)], w=[("uz", ti)])
                        bm = bank()
                        for tcn in range(4):
                            op("pe", lambda e, bm=bm, tcn=tcn, co=co, cw=cw, g_=g_: e.matmul(
                                PS(bm)[:, tcn * 128:(tcn + 1) * 128], lhsT=vv[:, tcn, co:co + 128], rhs=wsT[:, g_, :],
                                start=True, stop=True), r=[("vv", tcn), "wsT"], w=[("ps", bm)])
                        op("dve", lambda e, bm=bm, ti=ti, cw=cw, jc=jc: e.scalar_tensor_tensor(
                            out=tb[0:cw, ti, :].rearrange("p (c t) -> p c t", c=4),
                            in0=PS(bm)[0:cw, :].rearrange("p (c t) -> p c t", c=4),
                            scalar=params[0:cw, PC_LNWC + jc:PC_LNWC + jc + 1],
                            in1=Cc[0:cw, jc, :].unsqueeze(1).broadcast_to([cw, 4, 128]),
                            op0=ALU.mult, op1=ALU.add), r=[("ps", bm), "params", "Cc"], w=[("tb", ti)])
                        op("dve", lambda e, ti=ti, cw=cw, jc=jc: e.tensor_tensor(out=yaT[0:cw, jc, :], in0=tb[0:cw, ti, :],
                                                                               in1=uz[0:cw, ti, :], op=ALU.mult),
                           r=[("tb", ti), ("uz", ti)], w=[("yaT", jc)])
                    if debug and b == 0 and st == 0:
                        dbg_store("yaT", yaT[:].rearrange("p k t -> p (k t)"), [("yaT", i) for i in range(8)], 4096)

                    def merge_branch(br, gnames, pfun, first, last):
                        wg = [wget(gnames[0]), wget(gnames[1])]
                        pinfo = pfun()
                        for c in range(8):
                            half, cc = c // 4, c % 4
                            bg = bank()
                            mm_acc(PS(bg), [(wg[half][0][:, k, cc * 128:(cc + 1) * 128], hTs[:, k, :]) for k in range(8)],
                                   bg, [wg[half][1]] + hTs_all)
                            bp = bank()
                            pairs, preads = pinfo(half, cc)
                            mm_acc(PS(bp), pairs, bp, preads)
                            ti = t2()
                            op("act", lambda e, bg=bg, ti=ti, c=c: e.activation(
                                out=sg[:, ti, :], in_=PS(bg), func=AF.Sigmoid,
                                bias=params[:, PC_GATEB + 8 * br + c:PC_GATEB + 8 * br + c + 1], scale=1.0),
                               r=[("ps", bg), "params"], w=[("sg", ti)])
                            if first:
                                op("dve", lambda e, bp=bp, ti=ti, c=c: e.tensor_tensor(out=mg[:, c, :], in0=sg[:, ti, :],
                                                                                     in1=PS(bp), op=ALU.mult),
                                   r=[("sg", ti), ("ps", bp)], w=[("mg", c)])
                            else:
                                op("dve", lambda e, bp=bp, ti=ti: e.tensor_tensor(out=sg[:, ti, :], in0=sg[:, ti, :],
                                                                                in1=PS(bp), op=ALU.mult),
                                   r=[("sg", ti), ("ps", bp)], w=[("sg", ti)])
                                if last:
                                    op("pool", lambda e, ti=ti, c=c: e.tensor_tensor(out=mgb[:, c, :], in0=mg[:, c, :],
                                                                                     in1=sg[:, ti, :], op=ALU.add),
                                       r=[("sg", ti), ("mg", c)], w=[("mgb", c)])
                                else:
                                    op("pool", lambda e, ti=ti, c=c: e.tensor_tensor(out=mg[:, c, :], in0=mg[:, c, :],
                                                                                     in1=sg[:, ti, :], op=ALU.add),
                                       r=[("sg", ti), ("mg", c)], w=[("mg", c)])

                    def pfun_a():
                        pa = [wget("PA0"), wget("PA1")]

                        def info(half, cc):
                            pairs = []
                            for jc in range(8):
                                pairs.append((pa[half][0][:, jc, cc * 128:(cc + 1) * 128], yaT[:, jc, :]))
                            return pairs, [pa[half][1]] + [("yaT", i) for i in range(8)]
                        return info

                    if st < 3:
                        emit_hT_a(st + 1)
                    merge_branch(0, ("G0", "G1"), pfun_a, True, False)

                    wMQ, wMQr = wget("MQ")
                    wMZ, wMZr = wget("MZ")
                    for h in range(4):
                        bz = bank()
                        mm_acc(PS(bz), [(wMZ[:, k, h * 128:(h + 1) * 128], hTs[:, k, :]) for k in range(8)], bz, [wMZr] + hTs_all)
                        op("act", lambda e, bz=bz, h=h: e.activation(out=szm[:, h, :], in_=PS(bz), func=AF.Silu),
                           r=[("ps", bz)], w=[("szm", h)])

                    def m_front(h):
                        bq = bank()
                        mm_acc(PS(bq), [(wMQ[:, k, h * 128:(h + 1) * 128], hTs[:, k, :]) for k in range(8)], bq, [wMQr] + hTs_all)
                        ti = t2()
                        feat_norm(bq, 512, ones_bf[:], 1.0, 128 * EPS, PC_MQW, mqn[:, h, :], [("mqn", h)],
                                  tsq2[:, ti, :], tsr2[:, ti, :], "u%d" % ti)

                    def m_back(h):
                        pis = []
                        for mc in range(2):
                            bs = bank()
                            op("pe", lambda e, bs=bs, h=h, mc=mc: e.matmul(PS(bs), lhsT=mkT[:, h, mc * 128:(mc + 1) * 128],
                                                                           rhs=mqn[:, h, :], start=True, stop=True),
                               r=[("mkT", h), ("mqn", h)], w=[("ps", bs)])
                            pi = (2 * h + mc) % 4
                            op("act", lambda e, bs=bs, pi=pi: e.activation(out=pmT[:, pi, :], in_=PS(bs), func=AF.Exp),
                               r=[("ps", bs)], w=[("pm", pi)])
                            pis.append(pi)
                        by = bank()
                        mm_acc(PS(by), [(mv[:, mc, h * 128:(h + 1) * 128], pmT[:, pis[mc], :]) for mc in range(2)], by,
                               [("mv", 0), ("mv", 1), ("pm", pis[0]), ("pm", pis[1])])
                        bl = bank()
                        mm_acc(PS(bl), [(ones_bf[:], pmT[:, pis[mc], :]) for mc in range(2)], bl,
                               ["ones_bf", ("pm", pis[0]), ("pm", pis[1])])
                        ti = h % 2
                        op("act", lambda e, bl=bl, ti=ti: e.activation(out=rl[:, ti, :], in_=PS(bl), func=AF.Ln), r=[("ps", bl)], w=[("rl", ti)])
                        op("act", lambda e, ti=ti: e.activation(out=rl[:, ti, :], in_=rl[:, ti, :], func=AF.Exp, scale=-1.0),
                           r=[("rl", ti)], w=[("rl", ti)])
                        op("dve", lambda e, by=by, ti=ti: e.tensor_tensor(out=rl[:, ti, :], in0=rl[:, ti, :], in1=PS(by), op=ALU.mult),
                           r=[("ps", by), ("rl", ti)], w=[("rl", ti)])
                        op("pool", lambda e, ti=ti, h=h: e.tensor_tensor(out=ymT[:, h, :], in0=rl[:, ti, :], in1=szm[:, h, :], op=ALU.mult),
                           r=[("rl", ti), ("szm", h)], w=[("ymT", h)])

                    m_front(0)
                    m_front(1)
                    m_back(0)
                    m_front(2)
                    m_back(1)
                    m_front(3)
                    m_back(2)
                    m_back(3)
                    if debug and b == 0 and st == 0:
                        dbg_store("ymT", ymT[:].rearrange("p k t -> p (k t)"), [("ymT", i) for i in range(4)], 2048)

                    def pfun_b():
                        pb = wget("PB")

                        def info(half, cc):
                            pairs = [(pb[0][:, 4 * half + h, cc * 128:(cc + 1) * 128], ybT[:, h, T0:T0 + 512]) for h in range(4)]
                            return pairs, [pb[1], ("ybT", st)]
                        return info

                    def pfun_m():
                        pmw = wget("PM")

                        def info(half, cc):
                            pairs = [(pmw[0][:, 4 * half + h, cc * 128:(cc + 1) * 128], ymT[:, h, :]) for h in range(4)]
                            return pairs, [pmw[1]] + [("ymT", i) for i in range(4)]
                        return info

                    if st < 3:
                        emit_hT_b(st + 1)
                    merge_branch(1, ("G2", "G3"), pfun_b, False, False)
                    merge_branch(2, ("G4", "G5"), pfun_m, False, True)
                    if debug and b == 0 and st == 0:
                        dbg_store("mg", mgb[:].rearrange("p k t -> p (k t)"), [("mgb", i) for i in range(8)], 4096)

                    wo = [wget("WO0"), wget("WO1")]
                    for j in range(4):
                        rj = j % 2
                        xsrc = x_d[b, T0 + j * 128:T0 + (j + 1) * 128, :]
                        odst = out_d[b, T0 + j * 128:T0 + (j + 1) * 128, :]
                        op("sp", lambda e, rj=rj, xsrc=xsrc: e.dma_start(out=res[:, rj, :], in_=xsrc),
                           w=[("res", rj)], dma="res%d" % rj)
                        for half in range(2):
                            bo = bank()
                            mm_acc(PS(bo), [(mgb[:, c, j * 128:(j + 1) * 128], wo[half][0][:, c, :]) for c in range(8)], bo,
                                   [wo[half][1]] + [("mgb", c) for c in range(8)])
                            op("dve", lambda e, bo=bo, rj=rj, half=half: e.tensor_tensor(
                                out=res[:, rj, half * 512:(half + 1) * 512], in0=res[:, rj, half * 512:(half + 1) * 512],
                                in1=PS(bo), op=ALU.add), r=[("ps", bo), ("res", rj)], w=[("res", rj)])
                        op("sp", lambda e, rj=rj, odst=odst: e.dma_start(out=odst, in_=res[:, rj, :]),
                           r=[("res", rj)], w=[], dma="res%d" % rj)
                op("sp", lambda e: e.nop(), r=[("dmatag", "res0"), ("dmatag", "res1"), ("dmatag", "dbg")], w=[])
                P.flush()
    return nc


def _prep_shared(inp):
    f = np.float32
    w_in = np.asarray(inp["w_in"], f)
    offs = np.cumsum([0, 768, 768, 768, 768, 768, 768, 256, 512, 512, 3072])
    a_u, a_v, a_z, b_q, b_k, b_v, b_z, m_q, m_z, g = [w_in[:, offs[i]:offs[i + 1]] for i in range(10)]
    img = np.zeros((NIMG, 1024, 512), f)

    def put(name, *cols):
        c = 0
        for a in cols:
            img[IMG[name], :, c:c + a.shape[1]] = a
            c += a.shape[1]

    for gi, nm in enumerate(("B0", "B2", "B3")):
        put(nm, b_q[:, 256 * gi:256 * gi + 256], b_k[:, 256 * gi:256 * gi + 256])
    put("B1", b_v[:, 0:256], b_v[:, 256:512])
    put("B4", b_z, b_v[:, 512:768])
    mwkv = np.asarray(inp["m_w_kv"], f)
    put("M0a", mwkv[:, 0:512])
    put("M0b", mwkv[:, 512:1024])
    put("V0", a_v[:, 0:384])
    put("V1", a_v[:, 384:768])
    put("U0", a_u[:, 0:512])
    put("U1", a_u[:, 512:768])
    put("Z0", a_z[:, 0:512])
    put("Z1", a_z[:, 512:768])
    put("MQ", m_q)
    put("MZ", m_z)
    for i in range(6):
        put("G%d" % i, g[:, 512 * i:512 * (i + 1)])
    pa = np.asarray(inp["proj_a"], f)
    for jc in range(8):
        g_, part = jc // 2, jc % 2
        cw = 128 if part == 0 else 64
        r0 = 192 * g_ + 128 * part
        for half in range(2):
            img[IMG["PA%d" % half], jc * 128:jc * 128 + cw, :] = pa[r0:r0 + cw, half * 512:(half + 1) * 512]
    pb = np.asarray(inp["proj_b"], f)
    pm = np.asarray(inp["proj_m"], f)
    for half in range(2):
        for h in range(4):
            img[IMG["PB"], (4 * half + h) * 128:(4 * half + h) * 128 + 64, :] = pb[64 * h:64 * h + 64, half * 512:(half + 1) * 512]
            img[IMG["PM"], (4 * half + h) * 128:(4 * half + h + 1) * 128, :] = pm[128 * h:128 * h + 128, half * 512:(half + 1) * 512]
    wo = np.asarray(inp["w_out"], f)
    put("WO0", wo[:, 0:512])
    put("WO1", wo[:, 512:1024])

    params = np.zeros((128, NP), f)
    params[:, PC_NORMW:PC_NORMW + 8] = np.asarray(inp["norm_w"], f).reshape(8, 128).T
    params[:, PC_MEMW:PC_MEMW + 8] = np.asarray(inp["mem_norm_w"], f).reshape(8, 128).T
    gb = np.asarray(inp["gate_b"], f)
    for br in range(3):
        params[:, PC_GATEB + 8 * br:PC_GATEB + 8 * br + 8] = gb[br].reshape(8, 128).T
    params[:, PC_BQW] = np.tile(np.asarray(inp["b_q_norm_w"], f), 2)
    params[:, PC_BKW] = np.tile(np.asarray(inp["b_k_norm_w"], f), 2)
    params[:, PC_MQW] = np.asarray(inp["m_q_norm_w"], f)
    params[:, PC_MKW] = np.asarray(inp["m_k_norm_w"], f)
    params[:, PC_NEGHALF] = -0.5
    params[:, PC_BS:PC_BS + 512] = np.asarray(inp["a_spatial_b"], f).reshape(1, 512)
    lnw = np.asarray(inp["a_v_norm_w"], f)
    lnb = np.asarray(inp["a_v_norm_b"], f)
    for jc in range(8):
        cw = 128 if jc % 2 == 0 else 64
        co = 192 * (jc // 2) + 128 * (jc % 2)
        params[0:cw, PC_LNWC + jc] = lnw[co:co + cw]
        params[0:cw, PC_LNBC + jc] = lnb[co:co + cw]

    wst = np.ascontiguousarray(np.transpose(np.asarray(inp["a_spatial_w"], f), (2, 0, 1))).reshape(128, 512)

    cst = np.zeros((128, 512), f)
    cst[:, 0:128] = np.eye(128, dtype=f)
    blk = np.zeros((128, 128), f)
    blk[0:64, 0:64] = 1
    blk[64:, 64:] = 1
    cst[:, 128:256] = blk
    kj = np.arange(128)[:, None]
    qi = np.arange(128)[None, :]
    cst[:, 256:384] = (kj >= qi).astype(f)
    cst[:, 384:512] = (kj <= qi).astype(f)

    rb = np.asarray(inp["rel_bias"], f)
    biasT = np.zeros((5, 128, 512), f)
    tiles = [(0, 0), (0, 1), (1, 0), (1, 1), (2, 1)]
    for idx, (gi, cur) in enumerate(tiles):
        d = (1, 4, 16)[gi]
        step = (qi - kj) if cur else (qi + 128 - kj)
        bk = _bucket(np.maximum(step, 0) * d)
        for h in range(4):
            biasT[idx, :, h * 128:(h + 1) * 128] = rb[bk, 4 * gi + h]
    return dict(wimg=img, params=params, wst=wst, cst=cst, biasT=biasT)


_CACHE = {}


def kernel(**inputs):
    inputs = dict(inputs)
    debug = bool(inputs.pop("_debug", False))
    _nc_override = inputs.pop("_ncores", None)
    shared = _prep_shared(inputs)
    x = np.ascontiguousarray(np.asarray(inputs["x"], np.float32))
    mem = np.ascontiguousarray(np.asarray(inputs["mem"], np.float32))
    stop = int(inputs.pop("_stop", 99))
    bstop = int(inputs.pop("_bstop", 99))
    key = ("nc", debug, stop, bstop)
    if key not in _CACHE:
        _CACHE[key] = build(debug, stop, bstop)
    nc = _CACHE[key]
    ncores = int(_nc_override) if _nc_override is not None else NCORES
    in_maps = []
    for c in range(ncores):
        m = dict(shared)
        m["x"] = x[c * BPC:(c + 1) * BPC]
        m["mem"] = mem[c * BPC:(c + 1) * BPC]
        in_maps.append(m)
    r = run_bass_kernel_spmd(nc, in_maps, core_ids=list(range(ncores)))
    out = np.concatenate([np.asarray(rr["out"]) for rr in r.results], axis=0).astype(np.float32)
    if debug:
        return out, r.results
    return out
```

```python
import contextlib
import os
import numpy as np
import concourse.bass as bass
import concourse.mybir as mybir
from concourse.bass_utils import run_bass_kernel_spmd

F32 = mybir.dt.float32
BF16 = mybir.dt.bfloat16
AF = mybir.ActivationFunctionType
ALU = mybir.AluOpType

NCORES = 8
BPC = 4
S = 2048
DM = 1024
EPS = 1e-6
NSLOT = 6
LOOKAHEAD = 2
SAME_SYNC = os.environ.get("KSAME", "1") == "1"

IMG = {}
_names = ["B0", "B1", "B2", "B3", "B4", "M0a", "M0b", "V0", "V1", "U0", "Z0", "U1", "Z1",
          "MQ", "MZ", "G0", "G1", "G2", "G3", "G4", "G5", "PA0", "PA1", "PB", "PM", "WO0", "WO1"]
for _i, _n in enumerate(_names):
    IMG[_n] = _i
NIMG = len(_names)

PC_NORMW = 0
PC_MEMW = 8
PC_GATEB = 16
PC_BQW = 40
PC_BKW = 41
PC_MQW = 42
PC_MKW = 43
PC_NEGHALF = 44
PC_BS = 48
PC_LNWC = 560
PC_LNBC = 568
NP = 576


def _bucket(dist):
    dist = dist.astype(np.int32)
    max_exact = 16
    is_small = dist < max_exact
    df = np.maximum(dist, 1).astype(np.float32)
    large = max_exact + (np.log(df / np.float32(max_exact)) / np.float32(np.log(2048 / max_exact))
                         * np.float32(32 - max_exact)).astype(np.int32)
    large = np.minimum(large, 31)
    return np.where(is_small, dist, large)


class Prog:
    ENGS = ["pe", "act", "dve", "pool", "sp"]

    def __init__(self, nc, stack):
        self.nc = nc
        self.stack = stack
        self.eng = {"pe": nc.tensor, "act": nc.scalar, "dve": nc.vector, "pool": nc.gpsimd, "sp": nc.sync}
        self.sem = {}
        for e in self.ENGS:
            self.sem[("eng", e)] = stack.enter_context(nc.semaphore("sem_" + e))
        self.cnt = {}
        self.pending = {e: [] for e in self.ENGS}
        self.known = {e: {} for e in self.ENGS}
        self.last_writer = {}
        self.readers = {}
        self.nblocks = 0

    def _sem(self, key):
        if key not in self.sem:
            self.sem[key] = self.stack.enter_context(self.nc.semaphore("sem_%s" % (key[1],)))
        return self.sem[key]

    def op(self, eng, fn, r=(), w=(), dma=None):
        deps = set()
        w = list(w)
        if dma is not None:
            w.append(("dmatag", dma))
        for x in r:
            if x in self.last_writer:
                deps.add(self.last_writer[x])
        for x in w:
            if x in self.last_writer:
                deps.add(self.last_writer[x])
            for t in self.readers.get(x, ()):
                deps.add(t)
        if dma is None:
            key = ("eng", eng)
            self.cnt[key] = self.cnt.get(key, 0) + 1
        else:
            key = ("dma", dma)
            self._sem(key)
            self.cnt[key] = self.cnt.get(key, 0) + 16
        tok = (key, self.cnt[key])
        waits = {}
        for (k, v) in deps:
            if k == ("eng", eng) and (eng == "pe" or not SAME_SYNC) and dma is None:
                continue
            waits[k] = max(waits.get(k, 0), v)
        known = self.known[eng]
        final = []
        for k, v in waits.items():
            if known.get(k, 0) < v:
                final.append((k, v))
                known[k] = v
        self.pending[eng].append((final, fn, tok, dma is not None))
        for x in w:
            self.last_writer[x] = tok
            self.readers[x] = []
        for x in r:
            self.readers.setdefault(x, []).append(tok)
        return tok

    def flush(self):
        nc = self.nc
        self.nblocks += 1
        with nc.Block() as block:
            for name in self.ENGS:
                lst = self.pending[name]
                if not lst:
                    continue

                def body(e, lst=lst):
                    for waits, fn, tok, isdma in lst:
                        for k, v in waits:
                            e.wait_ge(self.sem[k], v)
                        ins = fn(e)
                        ins.then_inc(self.sem[tok[0]], 16 if isdma else 1)

                {"pe": block.tensor, "act": block.scalar, "dve": block.vector,
                 "pool": block.gpsimd, "sp": block.sync}[name](body)
        self.pending = {e: [] for e in self.ENGS}


def build(debug=False, stop=99, bstop=99):
    split_b = os.environ.get('KSPLIT', '0') == '1'
    nc = bass.Bass("TRN2", target_bir_lowering=False)
    x_d = nc.dram_tensor("x", [BPC, S, DM], F32, kind="ExternalInput").ap()
    mem_d = nc.dram_tensor("mem", [BPC, 256, DM], F32, kind="ExternalInput").ap()
    wimg_d = nc.dram_tensor("wimg", [NIMG, 1024, 512], F32, kind="ExternalInput").ap()
    params_d = nc.dram_tensor("params", [128, NP], F32, kind="ExternalInput").ap()
    wst_d = nc.dram_tensor("wst", [128, 512], F32, kind="ExternalInput").ap()
    cst_d = nc.dram_tensor("cst", [128, 512], F32, kind="ExternalInput").ap()
    bias_d = nc.dram_tensor("biasT", [5, 128, 512], F32, kind="ExternalInput").ap()
    out_d = nc.dram_tensor("out", [BPC, S, DM], F32, kind="ExternalOutput").ap()
    wscr_d = nc.dram_tensor("wscr", [NIMG, 128, 4096], BF16, kind="Internal").ap()
    dbg = {}
    if debug:
        dbg["hT"] = nc.dram_tensor("dbg_hT", [128, 8 * 512], BF16, kind="ExternalOutput").ap()
        dbg["ybT"] = nc.dram_tensor("dbg_ybT", [64, 4 * 2048], BF16, kind="ExternalOutput").ap()
        dbg["yaT"] = nc.dram_tensor("dbg_yaT", [128, 8 * 512], BF16, kind="ExternalOutput").ap()
        dbg["ymT"] = nc.dram_tensor("dbg_ymT", [128, 4 * 512], BF16, kind="ExternalOutput").ap()
        dbg["mg"] = nc.dram_tensor("dbg_mg", [128, 8 * 512], BF16, kind="ExternalOutput").ap()

    with contextlib.ExitStack() as stack:
        P = Prog(nc, stack)
        op = P.op

        uniq = [0]

        def un(name):
            uniq[0] += 1
            return "s%d_%s" % (uniq[0], name)

        def sb(name, shape, dt):
            return stack.enter_context(nc.sbuf_tensor(un(name), shape, dt))

        params = sb("params", [128, NP], F32)
        ident = sb("ident", [128, 128], BF16)
        blk64 = sb("blk64", [128, 128], BF16)
        ones_bf = sb("ones_bf", [128, 128], BF16)
        sel64 = sb("sel64", [128, 128], BF16)
        wsT = sb("wsT", [128, 4, 128], BF16)
        Cc = sb("Cc", [128, 8, 128], F32)
        E = sb("E", [128, 5, 512], BF16)
        ring = sb("ring", [128, NSLOT, 8, 512], BF16)
        xb = sb("xb", [128, 2, 1024], F32)
        xn = sb("xn", [128, 4, 1024], BF16)
        ybT = sb("ybT", [128, 4, 2048], BF16)
        mkT = sb("mkT", [128, 4, 256], BF16)
        mv = sb("mv", [128, 2, 512], BF16)
        stat = sb("stat", [128, 64], F32)
        ps = stack.enter_context(nc.psum_tensor("psum_all", [128, 8, 512], F32))

        bankctr = [0]

        def bank():
            i = bankctr[0] % 8
            bankctr[0] += 1
            return i

        def PS(i):
            return ps[:, i, :]

        uses = []
        for b in range(BPC):
            uses += ["M0a", "M0b", "B0", "B1", "B2", "B3", "B4"]
            for st in range(4):
                uses += ["V0", "V1", "U0", "Z0", "U1", "Z1", "G0", "G1", "PA0", "PA1",
                         "MQ", "MZ", "G2", "G3", "PB", "G4", "G5", "PM", "WO0", "WO1"]
        wstate = {"issued": 0, "ptr": 0, "cached": set()}

        def w_issue(n):
            name = uses[n]
            slot = n % NSLOT
            img = IMG[name]
            dst = ring[:, slot, :, :]
            dflat = dst.rearrange("p k c -> p (k c)")
            if name in wstate["cached"]:
                op("sp", lambda e, dflat=dflat, img=img: e.dma_start(out=dflat, in_=wscr_d[img]),
                   r=[("wscr", img)], w=[("ring", slot)], dma="ringH%d" % slot)
            else:
                src = wimg_d[img].rearrange("(k p) c -> p k c", p=128)
                op("pool", lambda e, dst=dst, src=src: e.dma_start(out=dst, in_=src),
                   w=[("ring", slot)], dma="ring%d" % slot)
                op("sp", lambda e, dflat=dflat, img=img: e.dma_start(out=wscr_d[img], in_=dflat),
                   r=[("ring", slot)], w=[("wscr", img)], dma="wst%d" % slot)
                wstate["cached"].add(name)

        def wget(name):
            n = wstate["ptr"]
            assert uses[n] == name, (uses[n], name, n)
            wstate["ptr"] += 1
            while wstate["issued"] < min(len(uses), n + 1 + LOOKAHEAD):
                w_issue(wstate["issued"])
                wstate["issued"] += 1
            slot = n % NSLOT
            return ring[:, slot, :, :], ("ring", slot)

        op("sp", lambda e: e.dma_start(out=params[:], in_=params_d), w=["params"], dma="params")
        op("pool", lambda e: e.memset(ones_bf[:], 1.0), w=["ones_bf"])
        op("pool", lambda e: e.memset(ybT[:], 0.0), w=[("ybT", i_) for i_ in range(4)])
        with contextlib.ExitStack() as phs:
            cstf = phs.enter_context(nc.sbuf_tensor(un("cstf"), [128, 512], F32))
            wstf = phs.enter_context(nc.sbuf_tensor(un("wstf"), [128, 512], F32))
            bt = phs.enter_context(nc.sbuf_tensor(un("bt"), [128, 512], F32))
            bt2 = phs.enter_context(nc.sbuf_tensor(un("bt2"), [128, 512], F32))
            op("sp", lambda e: e.dma_start(out=cstf[:], in_=cst_d), w=["cstf"], dma="cstf")
            op("sp", lambda e: e.dma_start(out=wstf[:], in_=wst_d), w=["wstf"], dma="wstf")
            op("dve", lambda e: e.tensor_copy(out=ident[:], in_=cstf[:, 0:128]), r=["cstf"], w=["ident"])
            op("dve", lambda e: e.tensor_copy(out=blk64[:], in_=cstf[:, 128:256]), r=["cstf"], w=["blk64"])
            op("dve", lambda e: e.tensor_copy(out=sel64[:], in_=cstf[:, 64:65].broadcast_to([128, 128])), r=["cstf"], w=["sel64"])
            op("dve", lambda e: e.tensor_tensor(
                out=wsT[:], in0=wstf[:].rearrange("p (g t) -> p g t", g=4),
                in1=cstf[:, 384:512].unsqueeze(1).broadcast_to([128, 4, 128]), op=ALU.mult),
               r=["wstf", "cstf"], w=["wsT"])
            brs = bank()
            op("pe", lambda e: e.matmul(PS(brs), lhsT=ones_bf[:], rhs=wsT[:].rearrange("p g t -> p (g t)"), start=True, stop=True),
               r=["ones_bf", "wsT"], w=[("ps", brs)])
            for jc in range(8):
                g_ = jc // 2
                op("dve", lambda e, jc=jc, g_=g_: e.scalar_tensor_tensor(
                    out=Cc[:, jc, :], in0=PS(brs)[:, g_ * 128:(g_ + 1) * 128], scalar=params[:, PC_LNBC + jc:PC_LNBC + jc + 1],
                    in1=params[:, PC_BS + 128 * g_:PC_BS + 128 * g_ + 128], op0=ALU.mult, op1=ALU.add),
                   r=[("ps", brs), "params"], w=["Cc"])
            for idx in range(5):
                mcol = 256 if idx in (0, 2) else 384
                op("sp", lambda e, idx=idx: e.dma_start(out=bt[:], in_=bias_d[idx]), w=["bt"], dma="bt")
                op("act", lambda e: e.activation(out=bt2[:], in_=bt[:], func=AF.Exp), r=["bt"], w=["bt2"])
                op("dve", lambda e, idx=idx, mcol=mcol: e.tensor_tensor(
                    out=E[:, idx, :].rearrange("p (h q) -> p h q", h=4),
                    in0=bt2[:].rearrange("p (h q) -> p h q", h=4),
                    in1=cstf[:, mcol:mcol + 128].unsqueeze(1).broadcast_to([128, 4, 128]), op=ALU.mult),
                   r=["bt2", "cstf"], w=[("E", idx)])
            P.flush()

        xctr = [0]

        def norm_a(src_dram, act_rstd=False):
            j = xctr[0] % 2
            sl = xctr[0] % 4
            xctr[0] += 1
            xt = xb[:, j, :]
            xnt = xn[:, sl, :]
            ssq = stat[:, 2 * sl:2 * sl + 1]
            rstd = stat[:, 2 * sl + 1:2 * sl + 2]
            op("sp", lambda e: e.dma_start(out=xt, in_=src_dram), w=[("xb", j)], dma="xb%d" % j)
            op("act", lambda e: e.activation(out=xnt, in_=xt, func=AF.Square, accum_out=ssq),
               r=[("xb", j)], w=[("xn", sl), ("ssq", sl)])
            if act_rstd:
                op("act", lambda e: e.activation(out=rstd, in_=ssq, func=AF.Ln, scale=1.0 / DM, bias=EPS),
                   r=[("ssq", sl)], w=[("rstd", sl)])
                op("act", lambda e: e.activation(out=rstd, in_=rstd, func=AF.Exp, scale=-0.5),
                   r=[("rstd", sl)], w=[("rstd", sl)])
            else:
                op("dve", lambda e: e.tensor_scalar(out=rstd, in0=ssq, scalar1=1.0 / DM, scalar2=EPS,
                                                    op0=ALU.mult, op1=ALU.add), r=[("ssq", sl)], w=[("rstd", sl)])
                op("pool", lambda e: e.tensor_tensor(out=rstd, in0=rstd, in1=params[:, PC_NEGHALF:PC_NEGHALF + 1],
                                                     op=ALU.pow), r=[("rstd", sl), "params"], w=[("rstd", sl)])
            op("dve", lambda e: e.tensor_scalar(out=xnt, in0=xt, scalar1=rstd, scalar2=None, op0=ALU.mult),
               r=[("xb", j), ("rstd", sl)], w=[("xn", sl)])
            return sl

        def norm_b(sl, wcol0, dst3, dst_res):
            xnt = xn[:, sl, :]
            bi = bank()
            pT = PS(bi).bitcast(BF16).rearrange("p (k t) -> p k t", k=8)
            for k in range(8):
                op("pe", lambda e, k=k: e.transpose(out=pT[:, k, :], in_=xnt[:, k * 128:(k + 1) * 128], identity=ident[:]),
                   r=[("xn", sl), "ident"], w=[("ps", bi)])
            op("dve", lambda e: e.tensor_tensor(
                out=dst3, in0=pT, in1=params[:, wcol0:wcol0 + 8].unsqueeze(2).broadcast_to([128, 8, 128]),
                op=ALU.mult), r=[("ps", bi), "params"], w=dst_res)

        def norm_transpose(src_dram, wcol0, dst3, dst_res):
            norm_b(norm_a(src_dram), wcol0, dst3, dst_res)

        def mm_acc(out_ap, pairs, bi, reads):
            n = len(pairs)
            for i, (l, rr) in enumerate(pairs):
                op("pe", lambda e, l=l, rr=rr, i=i: e.matmul(out_ap, lhsT=l, rhs=rr, start=(i == 0), stop=(i == n - 1)),
                   r=reads, w=[("ps", bi)])

        def feat_norm(bi, width, ones_l, sqrt_scale, sqrt_bias, wcol, out_ap, out_res, tmp_sq, tmp_sr, tmp_res, split=False):
            src = PS(bi)[:, 0:width]
            op("act", lambda e: e.activation(out=tmp_sq[:, 0:width], in_=src, func=AF.Square),
               r=[("ps", bi)], w=[tmp_res + "sq"])

            def tail():
                _feat_norm_tail(bi, width, ones_l, sqrt_scale, sqrt_bias, wcol, out_ap, out_res, tmp_sq, tmp_sr, tmp_res, src)
            if split:
                return tail
            tail()

        def _feat_norm_tail(bi, width, ones_l, sqrt_scale, sqrt_bias, wcol, out_ap, out_res, tmp_sq, tmp_sr, tmp_res, src):
            b2 = bank()
            op("pe", lambda e: e.matmul(PS(b2)[:, 0:width], lhsT=ones_l, rhs=tmp_sq[:, 0:width], start=True, stop=True),
               r=[tmp_res + "sq", "ones_bf", "blk64"], w=[("ps", b2)])
            op("act", lambda e: e.activation(out=tmp_sr[:, 0:width], in_=PS(b2)[:, 0:width], func=AF.Ln,
                                             scale=sqrt_scale, bias=sqrt_bias),
               r=[("ps", b2), "params"], w=[tmp_res + "sr"])
            op("act", lambda e: e.activation(out=tmp_sr[:, 0:width], in_=tmp_sr[:, 0:width], func=AF.Exp, scale=-0.5),
               r=[tmp_res + "sr"], w=[tmp_res + "sr"])
            op("dve", lambda e: e.scalar_tensor_tensor(out=out_ap, in0=src, scalar=params[:, wcol:wcol + 1],
                                                       in1=tmp_sr[:, 0:width], op0=ALU.mult, op1=ALU.mult),
               r=[("ps", bi), tmp_res + "sr", "params"], w=out_res)

        def dbg_store(name, src_ap, src_res, ncols, npart=128):
            if not debug:
                return
            op("sp", lambda e: e.dma_start(out=dbg[name], in_=src_ap), r=src_res, w=[], dma="dbg")


        for b in range(BPC):
            if stop <= 2 * b:
                break
            with contextlib.ExitStack() as phs:
                def pt(name, shape, dt):
                    return phs.enter_context(nc.sbuf_tensor(un(name), shape, dt))
                hT = pt("hT", [128, 8, 2048], BF16)
                qg = pt("qg", [128, 2, 2048], BF16)
                kg = pt("kg", [128, 2, 2048], BF16)
                vg = pt("vg", [128, 16, 4, 65], BF16)
                acc = pt("acc", [65, 4, 2048], F32)
                tsq = pt("tsq", [128, 2, 512], BF16)
                tsr = pt("tsr", [128, 2, 512], F32)
                ex = pt("ex", [128, 2, 512], BF16)
                pTt = pt("pT", [128, 4, 512], BF16)
                bszraw = pt("bszraw", [128, 1024], F32)
                memhT = bszraw[:].bitcast(BF16).rearrange("p (k m) -> p k m", k=8)
                bsz = bszraw[0:64, :].rearrange("p (a c) -> p a c", a=2)
                bt1 = tsr
                qm = pt("qm", [128, 4, 4, 128], BF16)
                op("pool", lambda e: e.memset(qm[:], 0.0), w=[("qm", i_) for i_ in range(4)])
                tctr = [0]
                pctr = [0]
                ptail = [None]

                def tmpi():
                    tctr[0] += 1
                    return tctr[0] % 2

                for mt in range(2):
                    norm_transpose(mem_d[b, mt * 128:(mt + 1) * 128, :], PC_MEMW,
                                   memhT[:, :, mt * 128:(mt + 1) * 128], [("memhT", mt)])
                wk, wkr = wget("M0a")
                for h in range(4):
                    bi = bank()
                    mm_acc(PS(bi)[:, 0:256], [(wk[:, k, h * 128:(h + 1) * 128], memhT[:, k, :]) for k in range(8)],
                           bi, [wkr, ("memhT", 0), ("memhT", 1)])
                    ti = tmpi()
                    feat_norm(bi, 256, ones_bf[:], 1.0 / 128, EPS, PC_MKW, mkT[:, h, :], [("mkT", h)],
                              tsq[:, ti, :], tsr[:, ti, :], "t%d" % ti)
                wv, wvr = wget("M0b")
                for mt in range(2):
                    bi = bank()
                    mm_acc(PS(bi), [(memhT[:, k, mt * 128:(mt + 1) * 128], wv[:, k, :]) for k in range(8)],
                           bi, [wvr, ("memhT", mt)])
                    op("act", lambda e, bi=bi, mt=mt: e.activation(out=mv[:, mt, :], in_=PS(bi), func=AF.Copy),
                       r=[("ps", bi)], w=[("mv", mt)])

                if bstop <= 1:
                    P.flush()
                    break
                hsl = {}
                for t in range(3):
                    hsl[t] = norm_a(x_d[b, t * 128:(t + 1) * 128, :], act_rstd=True)
                for t in range(16):
                    if t + 3 < 16:
                        hsl[t + 3] = norm_a(x_d[b, (t + 3) * 128:(t + 4) * 128, :], act_rstd=True)
                    norm_b(hsl[t], PC_NORMW, hT[:, :, t * 128:(t + 1) * 128], [("hT", t // 4, t % 4)])
                hT_all = [("hT", a, c) for a in range(4) for c in range(4)]
                op("pool", lambda e: e.memset(vg[:, :, :, 64:65], 1.0), w=["vg_ones"])

                def tok_view(ap3, g, blk512):
                    if g == 0:
                        return ap3[:, blk512 * 512:(blk512 + 1) * 512]
                    if g == 1:
                        return ap3[:, blk512:2048:4]
                    return ap3.rearrange("p (m r) -> p r m", r=16)[:, 4 * blk512:4 * blk512 + 4, :]

                def tok_view128(ap3, g, qb):
                    d = (1, 4, 16)[g]
                    nb = 16 // d
                    r, n = qb // nb, qb % nb
                    s0 = r + d * 128 * n
                    return ap3[:, s0:s0 + d * 127 + 1:d] if d > 1 else ap3[:, s0:s0 + 128]

                if bstop <= 2:
                    P.flush()
                    break
                if split_b:
                    P.flush()
                hblk = [xb[:].rearrange("p a c -> p (a c)").bitcast(BF16).rearrange("p (k t) -> p k t", k=8),
                        xn[:].rearrange("p a c -> p (a c)").rearrange("p (k t) -> p k t", k=8)]
                hbres = [[("xb", 0), ("xb", 1)], [("xn", 0), ("xn", 1), ("xn", 2), ("xn", 3)]]
                slotB = {}
                for g in range(3):
                    if bstop <= 3 + 2 * g:
                        break
                    d = (1, 4, 16)[g]
                    nb = 16 // d
                    wqk, wqkr = wget(("B0", "B2", "B3")[g])
                    slotB[g] = (wqk, wqkr)
                    if g == 0:
                        wv01, wv01r = wget("B1")
                    if g == 2:
                        wv2, wv2r = wget("B4")
                    if g < 2:
                        wvv, wvvr, vcol = wv01, wv01r, 256 * g
                    else:
                        wvv, wvvr, vcol = wv2, wv2r, 256
                    for blk in range(4):
                        if g > 0:
                            hb = hblk[blk % 2]
                            hbr = hbres[blk % 2]
                            if g == 1:
                                src_v = hT[:, :, blk:2048:4]
                                dst_v = hb
                            else:
                                src_v = hT[:].rearrange("p k (m r) -> p k r m", r=16)[:, :, 4 * blk:4 * blk + 4, :]
                                dst_v = hb.rearrange("p k (r m) -> p k r m", r=4)
                            op("pool", lambda e, src_v=src_v, dst_v=dst_v: e.tensor_copy(out=dst_v, in_=src_v),
                               r=hT_all, w=hbr)
                        for qk in range(2):
                            for pr in range(2):
                                bi = bank()
                                c0 = qk * 256 + pr * 128
                                hdeps = [("hT", blk, c_) for c_ in range(4)] if g == 0 else hbr
                                mm_acc(PS(bi), [(wqk[:, k, c0:c0 + 128], (tok_view(hT[:, k, :], g, blk) if g == 0 else hb[:, k, :]))
                                                for k in range(8)], bi, [wqkr] + hdeps)
                                ti = tmpi()
                                dst = (qg, kg)[qk]
                                if qk == 0:
                                    sc, bs_, wc = 1.0, 64 * EPS, PC_BQW
                                else:
                                    sc, bs_, wc = 1.0 / 64, EPS, PC_BKW
                                if ptail[0] is not None:
                                    ptail[0]()
                                ptail[0] = feat_norm(bi, 512, blk64[:], sc, bs_, wc, dst[:, pr, blk * 512:(blk + 1) * 512],
                                                     [(("qg", "kg")[qk], pr, blk)], tsq[:, ti, :], tsr[:, ti, :], "t%d" % ti,
                                                     split=True)
                        for half in range(2):
                            bi = bank()
                            for q2 in range(2):
                                qb = blk * 4 + half * 2 + q2
                                mm_acc(PS(bi)[:, q2 * 256:(q2 + 1) * 256],
                                       [((tok_view128(hT[:, k, :], g, qb) if g == 0 else hb[:, k, (qb % 4) * 128:(qb % 4 + 1) * 128]),
                                         wvv[:, k, vcol:vcol + 256]) for k in range(8)],
                                       bi, [wvvr] + ([("hT", qb // 4, qb % 4)] if g == 0 else hbr))
                                if ptail[0] is not None:
                                    ptail[0]()
                                    ptail[0] = None
                            qb0 = blk * 4 + half * 2
                            eng = "act" if half == 0 else "dve"
                            src = PS(bi).rearrange("p (q h c) -> p q h c", q=2, h=4)
                            dstv = vg[:, qb0:qb0 + 2, :, 0:64]
                            if eng == "act":
                                op("act", lambda e, src=src, dstv=dstv: e.activation(out=dstv, in_=src, func=AF.Copy),
                                   r=[("ps", bi)], w=[("vg", qb0), ("vg", qb0 + 1)])
                            else:
                                op("dve", lambda e, src=src, dstv=dstv: e.tensor_copy(out=dstv, in_=src),
                                   r=[("ps", bi)], w=[("vg", qb0), ("vg", qb0 + 1)])
                    if bstop <= 4 + 2 * g:
                        break
                    eidx = {0: (0, 1), 1: (2, 3), 2: (None, 4)}[g]
                    def q_copy(qc):
                        qs_ = qc % 4
                        op("dve", lambda e, qs_=qs_, qc=qc: e.tensor_copy(
                            out=qm[0:64, qs_, 0:4:2, :], in_=qg[0:64, :, qc * 128:(qc + 1) * 128]),
                           r=[("qg", 0, qc // 4), ("qg", 1, qc // 4)], w=[("qm", qs_)])
                        op("act", lambda e, qs_=qs_, qc=qc: e.activation(
                            out=qm[64:128, qs_, 1:4:2, :], in_=qg[64:128, :, qc * 128:(qc + 1) * 128], func=AF.Copy),
                           r=[("qg", 0, qc // 4), ("qg", 1, qc // 4)], w=[("qm", qs_)])

                    def att_front(qb):
                        r_, n_ = qb // nb, qb % nb
                        blks = []
                        if n_ > 0:
                            blks.append((qb - 1, eidx[0]))
                        blks.append((qb, eidx[1]))
                        pts = []
                        qs = qb % 4
                        for qc in ([0, 1, 2] if qb == 0 else ([qb + 2] if qb + 2 < 16 else [])):
                            q_copy(qc)
                        for (kb, ei) in blks:
                            bi = bank()
                            for h in range(4):
                                pr = h // 2
                                op("pe", lambda e, bi=bi, h=h, pr=pr, kb=kb, qs=qs: e.matmul(
                                    PS(bi)[:, h * 128:(h + 1) * 128],
                                    lhsT=kg[:, pr, kb * 128:(kb + 1) * 128],
                                    rhs=qm[:, qs, h, :], start=True, stop=True),
                                   r=[("kg", pr, kb // 4), ("qm", qs)], w=[("ps", bi)])
                            ti = tmpi()
                            op("act", lambda e, bi=bi, ti=ti: e.activation(out=ex[:, ti, :], in_=PS(bi), func=AF.Exp),
                               r=[("ps", bi)], w=[("ex", ti)])
                            pctr[0] += 1
                            pi = pctr[0] % 4
                            op("dve",
                               lambda e, ti=ti, pi=pi, ei=ei: e.tensor_tensor(out=pTt[:, pi, :], in0=ex[:, ti, :],
                                                                             in1=E[:, ei, :], op=ALU.mult),
                               r=[("ex", ti), ("E", ei)], w=[("pT", pi)])
                            pts.append((kb, pi))
                        return pts

                    def att_back(qb, pts):
                        r_, n_ = qb // nb, qb % nb
                        bi = bank()
                        for h in range(4):
                            for i, (kb, pi) in enumerate(pts):
                                op("pe", lambda e, bi=bi, h=h, kb=kb, pi=pi, i=i, n=len(pts): e.matmul(
                                    PS(bi)[0:65, h * 128:(h + 1) * 128], lhsT=vg[:, kb, h, :],
                                    rhs=pTt[:, pi, h * 128:(h + 1) * 128], start=(i == 0), stop=(i == n - 1)),
                                   r=[("vg", kb), "vg_ones", ("pT", pi)], w=[("ps", bi)])
                        s0 = r_ + d * 128 * n_
                        accv = acc[0:65, :, s0:s0 + d * 127 + 1:d] if d > 1 else acc[0:65, :, s0:s0 + 128]
                        pvv = PS(bi)[0:65, :].rearrange("p (h q) -> p h q", h=4)
                        if g == 0:
                            op("dve", lambda e, accv=accv, pvv=pvv: e.tensor_copy(out=accv, in_=pvv),
                               r=[("ps", bi)], w=["acc"])
                        else:
                            op("dve", lambda e, accv=accv, pvv=pvv: e.tensor_tensor(out=accv, in0=accv, in1=pvv, op=ALU.add),
                               r=[("ps", bi), "acc"], w=["acc"])

                    pend = None
                    for qb in range(16):
                        pts_ = att_front(qb)
                        if pend is not None:
                            att_back(*pend)
                        pend = (qb, pts_)
                    att_back(*pend)
                    if split_b and g < 2:
                        P.flush()
                if bstop <= 9:
                    P.flush()
                    break
                if split_b:
                    P.flush()
                bhl = pTt
                for blk in range(4):
                    for h in range(4):
                        bi = bank()
                        mm_acc(PS(bi), [(wv2[:, k, 64 * h:64 * h + 128], hT[:, k, blk * 512:(blk + 1) * 512])
                                        for k in range(8)], bi, [wv2r] + [("hT", blk, c_) for c_ in range(4)])
                        op("act", lambda e, bi=bi, h=h, blk=blk: e.activation(out=ybT[0:64, h, blk * 512:(blk + 1) * 512],
                                                                             in_=PS(bi)[0:64, :], func=AF.Silu),
                           r=[("ps", bi)], w=[("ybT", blk)])
                def fin_front(it, blk, h):
                    s_ = it % 2
                    asl = acc[0:65, h, blk * 512:(blk + 1) * 512]
                    op("act", lambda e, asl=asl, s_=s_: e.activation(out=bhl[0:65, 2 * s_, :], in_=asl, func=AF.Copy),
                       r=["acc"], w=[("pT", 2 * s_)])
                    op("dve", lambda e, asl=asl, s_=s_: e.tensor_tensor(out=bhl[0:65, 2 * s_ + 1, :], in0=asl, in1=bhl[0:65, 2 * s_, :],
                                                                      op=ALU.subtract),
                       r=["acc", ("pT", 2 * s_)], w=[("pT", 2 * s_ + 1)])
                    b2 = bank()
                    for i_ in range(2):
                        op("pe", lambda e, b2=b2, i_=i_, s_=s_: e.matmul(PS(b2), lhsT=sel64[:, :], rhs=bhl[:, 2 * s_ + i_, :],
                                                                        start=(i_ == 0), stop=(i_ == 1)),
                           r=[("pT", 2 * s_), ("pT", 2 * s_ + 1), "sel64"], w=[("ps", b2)])
                    return b2

                def fin_back(it, blk, h, b2):
                    ti = it % 2
                    op("act", lambda e, b2=b2, ti=ti: e.activation(out=bt1[0:64, ti, :], in_=PS(b2)[0:64, :], func=AF.Ln),
                       r=[("ps", b2)], w=["t%dsr" % ti])
                    op("act", lambda e, ti=ti: e.activation(out=bt1[0:64, ti, :], in_=bt1[0:64, ti, :], func=AF.Exp, scale=-1.0),
                       r=["t%dsr" % ti], w=["t%dsr" % ti])
                    op("dve", lambda e, ti=ti, h=h, blk=blk: e.tensor_tensor(
                        out=bt1[0:64, ti, :], in0=acc[0:64, h, blk * 512:(blk + 1) * 512], in1=bt1[0:64, ti, :], op=ALU.mult),
                       r=["acc", "t%dsr" % ti], w=["t%dsr" % ti])
                    op("pool", lambda e, ti=ti, h=h, blk=blk: e.tensor_tensor(
                        out=ybT[0:64, h, blk * 512:(blk + 1) * 512], in0=bt1[0:64, ti, :],
                        in1=ybT[0:64, h, blk * 512:(blk + 1) * 512], op=ALU.mult),
                       r=["t%dsr" % ti, ("ybT", blk)], w=[("ybT", blk)])

                fpend = None
                for it, (blk, h) in enumerate([(blk_, h_) for blk_ in range(4) for h_ in range(4)]):
                    b2_ = fin_front(it, blk, h)
                    if fpend is not None:
                        fin_back(*fpend)
                    fpend = (it, blk, h, b2_)
                fin_back(*fpend)
                if debug and b == 0:
                    dbg_store("ybT", ybT[0:64].rearrange("p h t -> p (h t)"), [("ybT", i) for i in range(4)], 8192, 64)
                P.flush()

            if stop <= 2 * b + 1:
                break
            with contextlib.ExitStack() as phs:
                def pt(name, shape, dt):
                    return phs.enter_context(nc.sbuf_tensor(un(name), shape, dt))
                hTs2 = pt("hTs", [128, 2, 8, 512], BF16)
                vv = pt("vv", [128, 4, 832], BF16)
                op("pool", lambda e: e.memset(vv[:], 0.0), w=[("vv", j_) for j_ in range(4)])
                gv = pt("gv", [128, 768], F32)
                bnst = pt("bnst", [128, 2, 6], F32)
                gu = pt("gu", [128, 2, 512], BF16)
                szt = pt("sz", [128, 2, 512], BF16)
                uz = pt("uz", [128, 2, 512], BF16)
                tb = pt("tb", [128, 2, 512], F32)
                yaT = pt("yaT", [128, 8, 512], BF16)
                op("pool", lambda e: e.memset(yaT[:], 0.0), w=[("yaT", j_) for j_ in range(8)])
                tsq2 = pt("tsq2", [128, 2, 512], BF16)
                tsr2 = pt("tsr2", [128, 2, 512], F32)
                mqn = pt("mqn", [128, 4, 512], BF16)
                szm = pt("szm", [128, 4, 512], BF16)
                pmT = pt("pm", [128, 4, 512], BF16)
                rl = pt("rl", [128, 2, 512], F32)
                ymT = pt("ymT", [128, 4, 512], BF16)
                sg = pt("sg", [128, 2, 512], F32)
                mg = pt("mg", [128, 8, 512], F32)
                mgb = pt("mgb", [128, 8, 512], BF16)
                res = pt("res", [128, 2, 1024], F32)
                tc2 = [0]

                def t2():
                    tc2[0] += 1
                    return tc2[0] % 2

                hslots = {}

                def emit_hT_a(st_):
                    hslots[st_] = [norm_a(x_d[b, st_ * 512 + j_ * 128:st_ * 512 + (j_ + 1) * 128, :]) for j_ in range(4)]

                def emit_hT_b(st_):
                    for j_ in range(4):
                        norm_b(hslots[st_][j_], PC_NORMW, hTs2[:, st_ % 2, :, j_ * 128:(j_ + 1) * 128], [("hTs", st_ % 2, j_)])

                emit_hT_a(0)
                emit_hT_b(0)
                for st in range(4):
                    T0 = st * 512
                    hTs = hTs2[:, st % 2]
                    hTs_all = [("hTs", st % 2, j) for j in range(4)]
                    if debug and b == 0 and st == 0:
                        dbg_store("hT", hTs.rearrange("p k t -> p (k t)"), hTs_all, 4096)
                    wV0, wV0r = wget("V0")
                    wV1, wV1r = wget("V1")
                    for j in range(4):
                        for half in range(2):
                            wv_, wvr_ = (wV0, wV0r) if half == 0 else (wV1, wV1r)
                            bi = bank()
                            mm_acc(PS(bi)[:, 0:384], [(hTs[:, k, j * 128:(j + 1) * 128], wv_[:, k, 0:384]) for k in range(8)],
                                   bi, [wvr_, ("hTs", st % 2, j)])
                            op("act", lambda e, bi=bi, half=half: e.activation(out=gv[:, half * 384:(half + 1) * 384],
                                                                               in_=PS(bi)[:, 0:384], func=AF.Gelu),
                               r=[("ps", bi)], w=[("gv", half)])
                            op("dve", lambda e, half=half: e.bn_stats(out=bnst[:, half, :], in_=gv[:, half * 384:(half + 1) * 384]),
                               r=[("gv", half)], w=[("bnst", half)])
                        mvv = stat[:, 8:10]
                        rs = stat[:, 10:11]
                        op("dve", lambda e: e.bn_aggr(out=mvv, in_=bnst[:].rearrange("p a b -> p (a b)")),
                           r=[("bnst", 0), ("bnst", 1)], w=["mvv"])
                        op("dve", lambda e: e.tensor_scalar(out=rs, in0=stat[:, 9:10], scalar1=EPS, scalar2=None, op0=ALU.add),
                           r=["mvv"], w=["rs"])
                        op("pool", lambda e: e.tensor_tensor(out=rs, in0=rs, in1=params[:, PC_NEGHALF:PC_NEGHALF + 1], op=ALU.pow),
                           r=["rs", "params"], w=["rs"])
                        op("dve", lambda e, j=j: e.tensor_scalar(out=vv[:, j, 0:768], in0=gv[:], scalar1=stat[:, 8:9], scalar2=rs,
                                                            op0=ALU.subtract, op1=ALU.mult),
                           r=[("gv", 0), ("gv", 1), "mvv", "rs"], w=[("vv", j)])
                    wU = {}
                    for jc in range(8):
                        g_, part = jc // 2, jc % 2
                        cw = 128 if part == 0 else 64
                        co = 192 * g_ + 128 * part
                        if jc == 0:
                            wU[0] = wget("U0")
                            wU[1] = wget("Z0")
                        if jc == 5:
                            wU[0] = wget("U1")
                            wU[1] = wget("Z1")
                        c0 = co if co < 512 else co - 512
                        (wu, wur), (wz, wzr) = wU[0], wU[1]
                        bu = bank()
                        mm_acc(PS(bu), [(wu[:, k, c0:c0 + 128], hTs[:, k, :]) for k in range(8)], bu, [wur] + hTs_all)
                        bz = bank()
                        mm_acc(PS(bz), [(wz[:, k, c0:c0 + 128], hTs[:, k, :]) for k in range(8)], bz, [wzr] + hTs_all)
                        ti = t2()
                        op("act", lambda e, bu=bu, ti=ti, cw=cw: e.activation(out=gu[0:cw, ti, :], in_=PS(bu)[0:cw, :], func=AF.Gelu),
                           r=[("ps", bu)], w=[("gu", ti)])
                        op("act", lambda e, bz=bz, ti=ti, cw=cw: e.activation(out=szt[0:cw, ti, :], in_=PS(bz)[0:cw, :], func=AF.Silu),
                           r=[("ps", bz)], w=[("sz", ti)])
                        op("pool", lambda e, ti=ti, cw=cw: e.tensor_tensor(out=uz[0:cw, ti, :], in0=gu[0:cw, ti, :],
                                                                            in1=szt[0:cw, ti, :], op=ALU.mult),
                           r=[("gu", ti), ("sz", ti)], w=[("uz", ti)])
                        bm = bank()
                        for tcn in range(4):
                            op("pe", lambda e, bm=bm, tcn=tcn, co=co, cw=cw, g_=g_: e.matmul(
                                PS(bm)[:, tcn * 128:(tcn + 1) * 128], lhsT=vv[:, tcn, co:co + 128], rhs=wsT[:, g_, :],
                                start=True, stop=True), r=[("vv", tcn), "wsT"], w=[("ps", bm)])
                        op("dve", lambda e, bm=bm, ti=ti, cw=cw, jc=jc: e.scalar_tensor_tensor(
                            out=tb[0:cw, ti, :].rearrange("p (c t) -> p c t", c=4),
                            in0=PS(bm)[0:cw, :].rearrange("p (c t) -> p c t", c=4),
                            scalar=params[0:cw, PC_LNWC + jc:PC_LNWC + jc + 1],
                            in1=Cc[0:cw, jc, :].unsqueeze(1).broadcast_to([cw, 4, 128]),
                            op0=ALU.mult, op1=ALU.add), r=[("ps", bm), "params", "Cc"], w=[("tb", ti)])
                        op("dve", lambda e, ti=ti, cw=cw, jc=jc: e.tensor_tensor(out=yaT[0:cw, jc, :], in0=tb[0:cw, ti, :],
                                                                               in1=uz[0:cw, ti, :], op=ALU.mult),
                           r=[("tb", ti), ("uz", ti)], w=[("yaT", jc)])
                    if debug and b == 0 and st == 0:
                        dbg_store("yaT", yaT[:].rearrange("p k t -> p (k t)"), [("yaT", i) for i in range(8)], 4096)

                    def merge_branch(br, gnames, pfun, first, last):
                        wg = [wget(gnames[0]), wget(gnames[1])]
                        pinfo = pfun()
                        for c in range(8):
                            half, cc = c // 4, c % 4
                            bg = bank()
                            mm_acc(PS(bg), [(wg[half][0][:, k, cc * 128:(cc + 1) * 128], hTs[:, k, :]) for k in range(8)],
                                   bg, [wg[half][1]] + hTs_all)
                            bp = bank()
                            pairs, preads = pinfo(half, cc)
                            mm_acc(PS(bp), pairs, bp, preads)
                            ti = t2()
                            op("act", lambda e, bg=bg, ti=ti, c=c: e.activation(
                                out=sg[:, ti, :], in_=PS(bg), func=AF.Sigmoid,
                                bias=params[:, PC_GATEB + 8 * br + c:PC_GATEB + 8 * br + c + 1], scale=1.0),
                               r=[("ps", bg), "params"], w=[("sg", ti)])
                            if first:
                                op("dve", lambda e, bp=bp, ti=ti, c=c: e.tensor_tensor(out=mg[:, c, :], in0=sg[:, ti, :],
                                                                                     in1=PS(bp), op=ALU.mult),
                                   r=[("sg", ti), ("ps", bp)], w=[("mg", c)])
                            else:
                                op("dve", lambda e, bp=bp, ti=ti: e.tensor_tensor(out=sg[:, ti, :], in0=sg[:, ti, :],
                                                                                in1=PS(bp), op=ALU.mult),
                                   r=[("sg", ti), ("ps", bp)], w=[("sg", ti)])
                                if last:
                                    op("pool", lambda e, ti=ti, c=c: e.tensor_tensor(out=mgb[:, c, :], in0=mg[:, c, :],
                                                                                     in1=sg[:, ti, :], op=ALU.add),
                                       r=[("sg", ti), ("mg", c)], w=[("mgb", c)])
                                else:
                                    op("pool", lambda e, ti=ti, c=c: e.tensor_tensor(out=mg[:, c, :], in0=mg[:, c, :],
                                                                                     in1=sg[:, ti, :], op=ALU.add),
                                       r=[("sg", ti), ("mg", c)], w=[("mg", c)])

                    def pfun_a():
                        pa = [wget("PA0"), wget("PA1")]

                        def info(half, cc):
                            pairs = []
                            for jc in range(8):
                                pairs.append((pa[half][0][:, jc, cc * 128:(cc + 1) * 128], yaT[:, jc, :]))
                            return pairs, [pa[half][1]] + [("yaT", i) for i in range(8)]
                        return info

                    if st < 3:
                        emit_hT_a(st + 1)
                    merge_branch(0, ("G0", "G1"), pfun_a, True, False)

                    wMQ, wMQr = wget("MQ")
                    wMZ, wMZr = wget("MZ")
                    for h in range(4):
                        bz = bank()
                        mm_acc(PS(bz), [(wMZ[:, k, h * 128:(h + 1) * 128], hTs[:, k, :]) for k in range(8)], bz, [wMZr] + hTs_all)
                        op("act", lambda e, bz=bz, h=h: e.activation(out=szm[:, h, :], in_=PS(bz), func=AF.Silu),
                           r=[("ps", bz)], w=[("szm", h)])

                    def m_front(h):
                        bq = bank()
                        mm_acc(PS(bq), [(wMQ[:, k, h * 128:(h + 1) * 128], hTs[:, k, :]) for k in range(8)], bq, [wMQr] + hTs_all)
                        ti = t2()
                        feat_norm(bq, 512, ones_bf[:], 1.0, 128 * EPS, PC_MQW, mqn[:, h, :], [("mqn", h)],
                                  tsq2[:, ti, :], tsr2[:, ti, :], "u%d" % ti)

                    def m_back(h):
                        pis = []
                        for mc in range(2):
                            bs = bank()
                            op("pe", lambda e, bs=bs, h=h, mc=mc: e.matmul(PS(bs), lhsT=mkT[:, h, mc * 128:(mc + 1) * 128],
                                                                           rhs=mqn[:, h, :], start=True, stop=True),
                               r=[("mkT", h), ("mqn", h)], w=[("ps", bs)])
                            pi = (2 * h + mc) % 4
                            op("act", lambda e, bs=bs, pi=pi: e.activation(out=pmT[:, pi, :], in_=PS(bs), func=AF.Exp),
                               r=[("ps", bs)], w=[("pm", pi)])
                            pis.append(pi)
                        by = bank()
                        mm_acc(PS(by), [(mv[:, mc, h * 128:(h + 1) * 128], pmT[:, pis[mc], :]) for mc in range(2)], by,
                               [("mv", 0), ("mv", 1), ("pm", pis[0]), ("pm", pis[1])])
                        bl = bank()
                        mm_acc(PS(bl), [(ones_bf[:], pmT[:, pis[mc], :]) for mc in range(2)], bl,
                               ["ones_bf", ("pm", pis[0]), ("pm", pis[1])])
                        ti = h % 2
                        op("act", lambda e, bl=bl, ti=ti: e.activation(out=rl[:, ti, :], in_=PS(bl), func=AF.Ln), r=[("ps", bl)], w=[("rl", ti)])
                        op("act", lambda e, ti=ti: e.activation(out=rl[:, ti, :], in_=rl[:, ti, :], func=AF.Exp, scale=-1.0),
                           r=[("rl", ti)], w=[("rl", ti)])
                        op("dve", lambda e, by=by, ti=ti: e.tensor_tensor(out=rl[:, ti, :], in0=rl[:, ti, :], in1=PS(by), op=ALU.mult),
                           r=[("ps", by), ("rl", ti)], w=[("rl", ti)])
                        op("pool", lambda e, ti=ti, h=h: e.tensor_tensor(out=ymT[:, h, :], in0=rl[:, ti, :], in1=szm[:, h, :], op=ALU.mult),
                           r=[("rl", ti), ("szm", h)], w=[("ymT", h)])

                    m_front(0)
                    m_front(1)
                    m_back(0)
                    m_front(2)
                    m_back(1)
                    m_front(3)
                    m_back(2)
                    m_back(3)
                    if debug and b == 0 and st == 0:
                        dbg_store("ymT", ymT[:].rearrange("p k t -> p (k t)"), [("ymT", i) for i in range(4)], 2048)

                    def pfun_b():
                        pb = wget("PB")

                        def info(half, cc):
                            pairs = [(pb[0][:, 4 * half + h, cc * 128:(cc + 1) * 128], ybT[:, h, T0:T0 + 512]) for h in range(4)]
                            return pairs, [pb[1], ("ybT", st)]
                        return info

                    def pfun_m():
                        pmw = wget("PM")

                        def info(half, cc):
                            pairs = [(pmw[0][:, 4 * half + h, cc * 128:(cc + 1) * 128], ymT[:, h, :]) for h in range(4)]
                            return pairs, [pmw[1]] + [("ymT", i) for i in range(4)]
                        return info

                    if st < 3:
                        emit_hT_b(st + 1)
                    merge_branch(1, ("G2", "G3"), pfun_b, False, False)
                    merge_branch(2, ("G4", "G5"), pfun_m, False, True)
                    if debug and b == 0 and st == 0:
                        dbg_store("mg", mgb[:].rearrange("p k t -> p (k t)"), [("mgb", i) for i in range(8)], 4096)

                    wo = [wget("WO0"), wget("WO1")]
                    for j in range(4):
                        rj = j % 2
                        xsrc = x_d[b, T0 + j * 128:T0 + (j + 1) * 128, :]
                        odst = out_d[b, T0 + j * 128:T0 + (j + 1) * 128, :]
                        op("sp", lambda e, rj=rj, xsrc=xsrc: e.dma_start(out=res[:, rj, :], in_=xsrc),
                           w=[("res", rj)], dma="res%d" % rj)
                        for half in range(2):
                            bo = bank()
                            mm_acc(PS(bo), [(mgb[:, c, j * 128:(j + 1) * 128], wo[half][0][:, c, :]) for c in range(8)], bo,
                                   [wo[half][1]] + [("mgb", c) for c in range(8)])
                            op("dve", lambda e, bo=bo, rj=rj, half=half: e.tensor_tensor(
                                out=res[:, rj, half * 512:(half + 1) * 512], in0=res[:, rj, half * 512:(half + 1) * 512],
                                in1=PS(bo), op=ALU.add), r=[("ps", bo), ("res", rj)], w=[("res", rj)])
                        op("sp", lambda e, rj=rj, odst=odst: e.dma_start(out=odst, in_=res[:, rj, :]),
                           r=[("res", rj)], w=[], dma="res%d" % rj)
                op("sp", lambda e: e.nop(), r=[("dmatag", "res0"), ("dmatag", "res1"), ("dmatag", "dbg")], w=[])
                P.flush()
    return nc


def _prep_shared(inp):
    f = np.float32
    w_in = np.asarray(inp["w_in"], f)
    offs = np.cumsum([0, 768, 768, 768, 768, 768, 768, 256, 512, 512, 3072])
    a_u, a_v, a_z, b_q, b_k, b_v, b_z, m_q, m_z, g = [w_in[:, offs[i]:offs[i + 1]] for i in range(10)]
    img = np.zeros((NIMG, 1024, 512), f)

    def put(name, *cols):
        c = 0
        for a in cols:
            img[IMG[name], :, c:c + a.shape[1]] = a
            c += a.shape[1]

    for gi, nm in enumerate(("B0", "B2", "B3")):
        put(nm, b_q[:, 256 * gi:256 * gi + 256], b_k[:, 256 * gi:256 * gi + 256])
    put("B1", b_v[:, 0:256], b_v[:, 256:512])
    put("B4", b_z, b_v[:, 512:768])
    mwkv = np.asarray(inp["m_w_kv"], f)
    put("M0a", mwkv[:, 0:512])
    put("M0b", mwkv[:, 512:1024])
    put("V0", a_v[:, 0:384])
    put("V1", a_v[:, 384:768])
    put("U0", a_u[:, 0:512])
    put("U1", a_u[:, 512:768])
    put("Z0", a_z[:, 0:512])
    put("Z1", a_z[:, 512:768])
    put("MQ", m_q)
    put("MZ", m_z)
    for i in range(6):
        put("G%d" % i, g[:, 512 * i:512 * (i + 1)])
    pa = np.asarray(inp["proj_a"], f)
    for jc in range(8):
        g_, part = jc // 2, jc % 2
        cw = 128 if part == 0 else 64
        r0 = 192 * g_ + 128 * part
        for half in range(2):
            img[IMG["PA%d" % half], jc * 128:jc * 128 + cw, :] = pa[r0:r0 + cw, half * 512:(half + 1) * 512]
    pb = np.asarray(inp["proj_b"], f)
    pm = np.asarray(inp["proj_m"], f)
    for half in range(2):
        for h in range(4):
            img[IMG["PB"], (4 * half + h) * 128:(4 * half + h) * 128 + 64, :] = pb[64 * h:64 * h + 64, half * 512:(half + 1) * 512]
            img[IMG["PM"], (4 * half + h) * 128:(4 * half + h + 1) * 128, :] = pm[128 * h:128 * h + 128, half * 512:(half + 1) * 512]
    wo = np.asarray(inp["w_out"], f)
    put("WO0", wo[:, 0:512])
    put("WO1", wo[:, 512:1024])

    params = np.zeros((128, NP), f)
    params[:, PC_NORMW:PC_NORMW + 8] = np.asarray(inp["norm_w"], f).reshape(8, 128).T
    params[:, PC_MEMW:PC_MEMW + 8] = np.asarray(inp["mem_norm_w"], f).reshape(8, 128).T
    gb = np.asarray(inp["gate_b"], f)
    for br in range(3):
        params[:, PC_GATEB + 8 * br:PC_GATEB + 8 * br + 8] = gb[br].reshape(8, 128).T
    params[:, PC_BQW] = np.tile(np.asarray(inp["b_q_norm_w"], f), 2)
    params[:, PC_BKW] = np.tile(np.asarray(inp["b_k_norm_w"], f), 2)
    params[:, PC_MQW] = np.asarray(inp["m_q_norm_w"], f)
    params[:, PC_MKW] = np.asarray(inp["m_k_norm_w"], f)
    params[:, PC_NEGHALF] = -0.5
    params[:, PC_BS:PC_BS + 512] = np.asarray(inp["a_spatial_b"], f).reshape(1, 512)
    lnw = np.asarray(inp["a_v_norm_w"], f)
    lnb = np.asarray(inp["a_v_norm_b"], f)
    for jc in range(8):
        cw = 128 if jc % 2 == 0 else 64
        co = 192 * (jc // 2) + 128 * (jc % 2)
        params[0:cw, PC_LNWC + jc] = lnw[co:co + cw]
        params[0:cw, PC_LNBC + jc] = lnb[co:co + cw]

    wst = np.ascontiguousarray(np.transpose(np.asarray(inp["a_spatial_w"], f), (2, 0, 1))).reshape(128, 512)

    cst = np.zeros((128, 512), f)
    cst[:, 0:128] = np.eye(128, dtype=f)
    blk = np.zeros((128, 128), f)
    blk[0:64, 0:64] = 1
    blk[64:, 64:] = 1
    cst[:, 128:256] = blk
    kj = np.arange(128)[:, None]
    qi = np.arange(128)[None, :]
    cst[:, 256:384] = (kj >= qi).astype(f)
    cst[:, 384:512] = (kj <= qi).astype(f)

    rb = np.asarray(inp["rel_bias"], f)
    biasT = np.zeros((5, 128, 512), f)
    tiles = [(0, 0), (0, 1), (1, 0), (1, 1), (2, 1)]
    for idx, (gi, cur) in enumerate(tiles):
        d = (1, 4, 16)[gi]
        step = (qi - kj) if cur else (qi + 128 - kj)
        bk = _bucket(np.maximum(step, 0) * d)
        for h in range(4):
            biasT[idx, :, h * 128:(h + 1) * 128] = rb[bk, 4 * gi + h]
    return dict(wimg=img, params=params, wst=wst, cst=cst, biasT=biasT)


_CACHE = {}


def kernel(**inputs):
    inputs = dict(inputs)
    debug = bool(inputs.pop("_debug", False))
    _nc_override = inputs.pop("_ncores", None)
    shared = _prep_shared(inputs)
    x = np.ascontiguousarray(np.asarray(inputs["x"], np.float32))
    mem = np.ascontiguousarray(np.asarray(inputs["mem"], np.float32))
    stop = int(inputs.pop("_stop", 99))
    bstop = int(inputs.pop("_bstop", 99))
    key = ("nc", debug, stop, bstop)
    if key not in _CACHE:
        _CACHE[key] = build(debug, stop, bstop)
    nc = _CACHE[key]
    ncores = int(_nc_override) if _nc_override is not None else NCORES
    in_maps = []
    for c in range(ncores):
        m = dict(shared)
        m["x"] = x[c * BPC:(c + 1) * BPC]
        m["mem"] = mem[c * BPC:(c + 1) * BPC]
        in_maps.append(m)
    r = run_bass_kernel_spmd(nc, in_maps, core_ids=list(range(ncores)))
    out = np.concatenate([np.asarray(rr["out"]) for rr in r.results], axis=0).astype(np.float32)
    if debug:
        return out, r.results
    return out
```

```python
import contextlib
import os
import numpy as np
import concourse.bass as bass
import concourse.mybir as mybir
from concourse.bass_utils import run_bass_kernel_spmd

F32 = mybir.dt.float32
BF16 = mybir.dt.bfloat16
AF = mybir.ActivationFunctionType
ALU = mybir.AluOpType

NCORES = 8
BPC = 4
S = 2048
DM = 1024
EPS = 1e-6
NSLOT = 6
LOOKAHEAD = 2
SAME_SYNC = os.environ.get("KSAME", "1") == "1"

IMG = {}
_names = ["B0", "B1", "B2", "B3", "B4", "M0a", "M0b", "V0", "V1", "U0", "Z0", "U1", "Z1",
          "MQ", "MZ", "G0", "G1", "G2", "G3", "G4", "G5", "PA0", "PA1", "PB", "PM", "WO0", "WO1"]
for _i, _n in enumerate(_names):
    IMG[_n] = _i
NIMG = len(_names)

PC_NORMW = 0
PC_MEMW = 8
PC_GATEB = 16
PC_BQW = 40
PC_BKW = 41
PC_MQW = 42
PC_MKW = 43
PC_NEGHALF = 44
PC_BS = 48
PC_LNWC = 560
PC_LNBC = 568
NP = 576


def _bucket(dist):
    dist = dist.astype(np.int32)
    max_exact = 16
    is_small = dist < max_exact
    df = np.maximum(dist, 1).astype(np.float32)
    large = max_exact + (np.log(df / np.float32(max_exact)) / np.float32(np.log(2048 / max_exact))
                         * np.float32(32 - max_exact)).astype(np.int32)
    large = np.minimum(large, 31)
    return np.where(is_small, dist, large)


class Prog:
    ENGS = ["pe", "act", "dve", "pool", "sp"]

    def __init__(self, nc, stack):
        self.nc = nc
        self.stack = stack
        self.eng = {"pe": nc.tensor, "act": nc.scalar, "dve": nc.vector, "pool": nc.gpsimd, "sp": nc.sync}
        self.sem = {}
        for e in self.ENGS:
            self.sem[("eng", e)] = stack.enter_context(nc.semaphore("sem_" + e))
        self.cnt = {}
        self.pending = {e: [] for e in self.ENGS}
        self.known = {e: {} for e in self.ENGS}
        self.last_writer = {}
        self.readers = {}
        self.nblocks = 0

    def _sem(self, key):
        if key not in self.sem:
            self.sem[key] = self.stack.enter_context(self.nc.semaphore("sem_%s" % (key[1],)))
        return self.sem[key]

    def op(self, eng, fn, r=(), w=(), dma=None):
        deps = set()
        w = list(w)
        if dma is not None:
            w.append(("dmatag", dma))
        for x in r:
            if x in self.last_writer:
                deps.add(self.last_writer[x])
        for x in w:
            if x in self.last_writer:
                deps.add(self.last_writer[x])
            for t in self.readers.get(x, ()):
                deps.add(t)
        if dma is None:
            key = ("eng", eng)
            self.cnt[key] = self.cnt.get(key, 0) + 1
        else:
            key = ("dma", dma)
            self._sem(key)
            self.cnt[key] = self.cnt.get(key, 0) + 16
        tok = (key, self.cnt[key])
        waits = {}
        for (k, v) in deps:
            if k == ("eng", eng) and (eng == "pe" or not SAME_SYNC) and dma is None:
                continue
            waits[k] = max(waits.get(k, 0), v)
        known = self.known[eng]
        final = []
        for k, v in waits.items():
            if known.get(k, 0) < v:
                final.append((k, v))
                known[k] = v
        self.pending[eng].append((final, fn, tok, dma is not None))
        for x in w:
            self.last_writer[x] = tok
            self.readers[x] = []
        for x in r:
            self.readers.setdefault(x, []).append(tok)
        return tok

    def flush(self):
        nc = self.nc
        self.nblocks += 1
        with nc.Block() as block:
            for name in self.ENGS:
                lst = self.pending[name]
                if not lst:
                    continue

                def body(e, lst=lst):
                    for waits, fn, tok, isdma in lst:
                        for k, v in waits:
                            e.wait_ge(self.sem[k], v)
                        ins = fn(e)
                        ins.then_inc(self.sem[tok[0]], 16 if isdma else 1)

                {"pe": block.tensor, "act": block.scalar, "dve": block.vector,
                 "pool": block.gpsimd, "sp": block.sync}[name](body)
        self.pending = {e: [] for e in self.ENGS}


def build(debug=False, stop=99, bstop=99):
    split_b = os.environ.get('KSPLIT', '0') == '1'
    nc = bass.Bass("TRN2", target_bir_lowering=False)
    x_d = nc.dram_tensor("x", [BPC, S, DM], F32, kind="ExternalInput").ap()
    mem_d = nc.dram_tensor("mem", [BPC, 256, DM], F32, kind="ExternalInput").ap()
    wimg_d = nc.dram_tensor("wimg", [NIMG, 1024, 512], F32, kind="ExternalInput").ap()
    params_d = nc.dram_tensor("params", [128, NP], F32, kind="ExternalInput").ap()
    wst_d = nc.dram_tensor("wst", [128, 512], F32, kind="ExternalInput").ap()
    cst_d = nc.dram_tensor("cst", [128, 512], F32, kind="ExternalInput").ap()
    bias_d = nc.dram_tensor("biasT", [5, 128, 512], F32, kind="ExternalInput").ap()
    out_d = nc.dram_tensor("out", [BPC, S, DM], F32, kind="ExternalOutput").ap()
    wscr_d = nc.dram_tensor("wscr", [NIMG, 128, 4096], BF16, kind="Internal").ap()
    dbg = {}
    if debug:
        dbg["hT"] = nc.dram_tensor("dbg_hT", [128, 8 * 512], BF16, kind="ExternalOutput").ap()
        dbg["ybT"] = nc.dram_tensor("dbg_ybT", [64, 4 * 2048], BF16, kind="ExternalOutput").ap()
        dbg["yaT"] = nc.dram_tensor("dbg_yaT", [128, 8 * 512], BF16, kind="ExternalOutput").ap()
        dbg["ymT"] = nc.dram_tensor("dbg_ymT", [128, 4 * 512], BF16, kind="ExternalOutput").ap()
        dbg["mg"] = nc.dram_tensor("dbg_mg", [128, 8 * 512], BF16, kind="ExternalOutput").ap()

    with contextlib.ExitStack() as stack:
        P = Prog(nc, stack)
        op = P.op

        uniq = [0]

        def un(name):
            uniq[0] += 1
            return "s%d_%s" % (uniq[0], name)

        def sb(name, shape, dt):
            return stack.enter_context(nc.sbuf_tensor(un(name), shape, dt))

        params = sb("params", [128, NP], F32)
        ident = sb("ident", [128, 128], BF16)
        blk64 = sb("blk64", [128, 128], BF16)
        ones_bf = sb("ones_bf", [128, 128], BF16)
        sel64 = sb("sel64", [128, 128], BF16)
        wsT = sb("wsT", [128, 4, 128], BF16)
        Cc = sb("Cc", [128, 8, 128], F32)
        E = sb("E", [128, 5, 512], BF16)
        ring = sb("ring", [128, NSLOT, 8, 512], BF16)
        xb = sb("xb", [128, 2, 1024], F32)
        xn = sb("xn", [128, 4, 1024], BF16)
        ybT = sb("ybT", [128, 4, 2048], BF16)
        mkT = sb("mkT", [128, 4, 256], BF16)
        mv = sb("mv", [128, 2, 512], BF16)
        stat = sb("stat", [128, 64], F32)
        ps = stack.enter_context(nc.psum_tensor("psum_all", [128, 8, 512], F32))

        bankctr = [0]

        def bank():
            i = bankctr[0] % 8
            bankctr[0] += 1
            return i

        def PS(i):
            return ps[:, i, :]

        uses = []
        for b in range(BPC):
            uses += ["M0a", "M0b", "B0", "B1", "B2", "B3", "B4"]
            for st in range(4):
                uses += ["V0", "V1", "U0", "Z0", "U1", "Z1", "G0", "G1", "PA0", "PA1",
                         "MQ", "MZ", "G2", "G3", "PB", "G4", "G5", "PM", "WO0", "WO1"]
        wstate = {"issued": 0, "ptr": 0, "cached": set()}

        def w_issue(n):
            name = uses[n]
            slot = n % NSLOT
            img = IMG[name]
            dst = ring[:, slot, :, :]
            dflat = dst.rearrange("p k c -> p (k c)")
            if name in wstate["cached"]:
                op("sp", lambda e, dflat=dflat, img=img: e.dma_start(out=dflat, in_=wscr_d[img]),
                   r=[("wscr", img)], w=[("ring", slot)], dma="ringH%d" % slot)
            else:
                src = wimg_d[img].rearrange("(k p) c -> p k c", p=128)
                op("pool", lambda e, dst=dst, src=src: e.dma_start(out=dst, in_=src),
                   w=[("ring", slot)], dma="ring%d" % slot)
                op("sp", lambda e, dflat=dflat, img=img: e.dma_start(out=wscr_d[img], in_=dflat),
                   r=[("ring", slot)], w=[("wscr", img)], dma="wst%d" % slot)
                wstate["cached"].add(name)

        def wget(name):
            n = wstate["ptr"]
            assert uses[n] == name, (uses[n], name, n)
            wstate["ptr"] += 1
            while wstate["issued"] < min(len(uses), n + 1 + LOOKAHEAD):
                w_issue(wstate["issued"])
                wstate["issued"] += 1
            slot = n % NSLOT
            return ring[:, slot, :, :], ("ring", slot)

        op("sp", lambda e: e.dma_start(out=params[:], in_=params_d), w=["params"], dma="params")
        op("pool", lambda e: e.memset(ones_bf[:], 1.0), w=["ones_bf"])
        op("pool", lambda e: e.memset(ybT[:], 0.0), w=[("ybT", i_) for i_ in range(4)])
        with contextlib.ExitStack() as phs:
            cstf = phs.enter_context(nc.sbuf_tensor(un("cstf"), [128, 512], F32))
            wstf = phs.enter_context(nc.sbuf_tensor(un("wstf"), [128, 512], F32))
            bt = phs.enter_context(nc.sbuf_tensor(un("bt"), [128, 512], F32))
            bt2 = phs.enter_context(nc.sbuf_tensor(un("bt2"), [128, 512], F32))
            op("sp", lambda e: e.dma_start(out=cstf[:], in_=cst_d), w=["cstf"], dma="cstf")
            op("sp", lambda e: e.dma_start(out=wstf[:], in_=wst_d), w=["wstf"], dma="wstf")
            op("dve", lambda e: e.tensor_copy(out=ident[:], in_=cstf[:, 0:128]), r=["cstf"], w=["ident"])
            op("dve", lambda e: e.tensor_copy(out=blk64[:], in_=cstf[:, 128:256]), r=["cstf"], w=["blk64"])
            op("dve", lambda e: e.tensor_copy(out=sel64[:], in_=cstf[:, 64:65].broadcast_to([128, 128])), r=["cstf"], w=["sel64"])
            op("dve", lambda e: e.tensor_tensor(
                out=wsT[:], in0=wstf[:].rearrange("p (g t) -> p g t", g=4),
                in1=cstf[:, 384:512].unsqueeze(1).broadcast_to([128, 4, 128]), op=ALU.mult),
               r=["wstf", "cstf"], w=["wsT"])
            brs = bank()
            op("pe", lambda e: e.matmul(PS(brs), lhsT=ones_bf[:], rhs=wsT[:].rearrange("p g t -> p (g t)"), start=True, stop=True),
               r=["ones_bf", "wsT"], w=[("ps", brs)])
            for jc in range(8):
                g_ = jc // 2
                op("dve", lambda e, jc=jc, g_=g_: e.scalar_tensor_tensor(
                    out=Cc[:, jc, :], in0=PS(brs)[:, g_ * 128:(g_ + 1) * 128], scalar=params[:, PC_LNBC + jc:PC_LNBC + jc + 1],
                    in1=params[:, PC_BS + 128 * g_:PC_BS + 128 * g_ + 128], op0=ALU.mult, op1=ALU.add),
                   r=[("ps", brs), "params"], w=["Cc"])
            for idx in range(5):
                mcol = 256 if idx in (0, 2) else 384
                op("sp", lambda e, idx=idx: e.dma_start(out=bt[:], in_=bias_d[idx]), w=["bt"], dma="bt")
                op("act", lambda e: e.activation(out=bt2[:], in_=bt[:], func=AF.Exp), r=["bt"], w=["bt2"])
                op("dve", lambda e, idx=idx, mcol=mcol: e.tensor_tensor(
                    out=E[:, idx, :].rearrange("p (h q) -> p h q", h=4),
                    in0=bt2[:].rearrange("p (h q) -> p h q", h=4),
                    in1=cstf[:, mcol:mcol + 128].unsqueeze(1).broadcast_to([128, 4, 128]), op=ALU.mult),
                   r=["bt2", "cstf"], w=[("E", idx)])
            P.flush()

        xctr = [0]

        def norm_a(src_dram, act_rstd=False):
            j = xctr[0] % 2
            sl = xctr[0] % 4
            xctr[0] += 1
            xt = xb[:, j, :]
            xnt = xn[:, sl, :]
            ssq = stat[:, 2 * sl:2 * sl + 1]
            rstd = stat[:, 2 * sl + 1:2 * sl + 2]
            op("sp", lambda e: e.dma_start(out=xt, in_=src_dram), w=[("xb", j)], dma="xb%d" % j)
            op("act", lambda e: e.activation(out=xnt, in_=xt, func=AF.Square, accum_out=ssq),
               r=[("xb", j)], w=[("xn", sl), ("ssq", sl)])
            if act_rstd:
                op("act", lambda e: e.activation(out=rstd, in_=ssq, func=AF.Ln, scale=1.0 / DM, bias=EPS),
                   r=[("ssq", sl)], w=[("rstd", sl)])
                op("act", lambda e: e.activation(out=rstd, in_=rstd, func=AF.Exp, scale=-0.5),
                   r=[("rstd", sl)], w=[("rstd", sl)])
            else:
                op("dve", lambda e: e.tensor_scalar(out=rstd, in0=ssq, scalar1=1.0 / DM, scalar2=EPS,
                                                    op0=ALU.mult, op1=ALU.add), r=[("ssq", sl)], w=[("rstd", sl)])
                op("pool", lambda e: e.tensor_tensor(out=rstd, in0=rstd, in1=params[:, PC_NEGHALF:PC_NEGHALF + 1],
                                                     op=ALU.pow), r=[("rstd", sl), "params"], w=[("rstd", sl)])
            op("dve", lambda e: e.tensor_scalar(out=xnt, in0=xt, scalar1=rstd, scalar2=None, op0=ALU.mult),
               r=[("xb", j), ("rstd", sl)], w=[("xn", sl)])
            return sl

        def norm_b(sl, wcol0, dst3, dst_res):
            xnt = xn[:, sl, :]
            bi = bank()
            pT = PS(bi).bitcast(BF16).rearrange("p (k t) -> p k t", k=8)
            for k in range(8):
                op("pe", lambda e, k=k: e.transpose(out=pT[:, k, :], in_=xnt[:, k * 128:(k + 1) * 128], identity=ident[:]),
                   r=[("xn", sl), "ident"], w=[("ps", bi)])
            op("dve", lambda e: e.tensor_tensor(
                out=dst3, in0=pT, in1=params[:, wcol0:wcol0 + 8].unsqueeze(2).broadcast_to([128, 8, 128]),
                op=ALU.mult), r=[("ps", bi), "params"], w=dst_res)

        def norm_transpose(src_dram, wcol0, dst3, dst_res):
            norm_b(norm_a(src_dram), wcol0, dst3, dst_res)

        def mm_acc(out_ap, pairs, bi, reads):
            n = len(pairs)
            for i, (l, rr) in enumerate(pairs):
                op("pe", lambda e, l=l, rr=rr, i=i: e.matmul(out_ap, lhsT=l, rhs=rr, start=(i == 0), stop=(i == n - 1)),
                   r=reads, w=[("ps", bi)])

        def feat_norm(bi, width, ones_l, sqrt_scale, sqrt_bias, wcol, out_ap, out_res, tmp_sq, tmp_sr, tmp_res, split=False):
            src = PS(bi)[:, 0:width]
            op("act", lambda e: e.activation(out=tmp_sq[:, 0:width], in_=src, func=AF.Square),
               r=[("ps", bi)], w=[tmp_res + "sq"])

            def tail():
                _feat_norm_tail(bi, width, ones_l, sqrt_scale, sqrt_bias, wcol, out_ap, out_res, tmp_sq, tmp_sr, tmp_res, src)
            if split:
                return tail
            tail()

        def _feat_norm_tail(bi, width, ones_l, sqrt_scale, sqrt_bias, wcol, out_ap, out_res, tmp_sq, tmp_sr, tmp_res, src):
            b2 = bank()
            op("pe", lambda e: e.matmul(PS(b2)[:, 0:width], lhsT=ones_l, rhs=tmp_sq[:, 0:width], start=True, stop=True),
               r=[tmp_res + "sq", "ones_bf", "blk64"], w=[("ps", b2)])
            op("act", lambda e: e.activation(out=tmp_sr[:, 0:width], in_=PS(b2)[:, 0:width], func=AF.Ln,
                                             scale=sqrt_scale, bias=sqrt_bias),
               r=[("ps", b2), "params"], w=[tmp_res + "sr"])
            op("act", lambda e: e.activation(out=tmp_sr[:, 0:width], in_=tmp_sr[:, 0:width], func=AF.Exp, scale=-0.5),
               r=[tmp_res + "sr"], w=[tmp_res + "sr"])
            op("dve", lambda e: e.scalar_tensor_tensor(out=out_ap, in0=src, scalar=params[:, wcol:wcol + 1],
                                                       in1=tmp_sr[:, 0:width], op0=ALU.mult, op1=ALU.mult),
               r=[("ps", bi), tmp_res + "sr", "params"], w=out_res)

        def dbg_store(name, src_ap, src_res, ncols, npart=128):
            if not debug:
                return
            op("sp", lambda e: e.dma_start(out=dbg[name], in_=src_ap), r=src_res, w=[], dma="dbg")


        for b in range(BPC):
            if stop <= 2 * b:
                break
            with contextlib.ExitStack() as phs:
                def pt(name, shape, dt):
                    return phs.enter_context(nc.sbuf_tensor(un(name), shape, dt))
                hT = pt("hT", [128, 8, 2048], BF16)
                qg = pt("qg", [128, 2, 2048], BF16)
                kg = pt("kg", [128, 2, 2048], BF16)
                vg = pt("vg", [128, 16, 4, 65], BF16)
                acc = pt("acc", [65, 4, 2048], F32)
                tsq = pt("tsq", [128, 2, 512], BF16)
                tsr = pt("tsr", [128, 2, 512], F32)
                ex = pt("ex", [128, 2, 512], BF16)
                pTt = pt("pT", [128, 4, 512], BF16)
                bszraw = pt("bszraw", [128, 1024], F32)
                memhT = bszraw[:].bitcast(BF16).rearrange("p (k m) -> p k m", k=8)
                bsz = bszraw[0:64, :].rearrange("p (a c) -> p a c", a=2)
                bt1 = tsr
                qm = pt("qm", [128, 4, 4, 128], BF16)
                op("pool", lambda e: e.memset(qm[:], 0.0), w=[("qm", i_) for i_ in range(4)])
                tctr = [0]
                pctr = [0]
                ptail = [None]

                def tmpi():
                    tctr[0] += 1
                    return tctr[0] % 2

                for mt in range(2):
                    norm_transpose(mem_d[b, mt * 128:(mt + 1) * 128, :], PC_MEMW,
                                   memhT[:, :, mt * 128:(mt + 1) * 128], [("memhT", mt)])
                wk, wkr = wget("M0a")
                for h in range(4):
                    bi = bank()
                    mm_acc(PS(bi)[:, 0:256], [(wk[:, k, h * 128:(h + 1) * 128], memhT[:, k, :]) for k in range(8)],
                           bi, [wkr, ("memhT", 0), ("memhT", 1)])
                    ti = tmpi()
                    feat_norm(bi, 256, ones_bf[:], 1.0 / 128, EPS, PC_MKW, mkT[:, h, :], [("mkT", h)],
                              tsq[:, ti, :], tsr[:, ti, :], "t%d" % ti)
                wv, wvr = wget("M0b")
                for mt in range(2):
                    bi = bank()
                    mm_acc(PS(bi), [(memhT[:, k, mt * 128:(mt + 1) * 128], wv[:, k, :]) for k in range(8)],
                           bi, [wvr, ("memhT", mt)])
                    op("act", lambda e, bi=bi, mt=mt: e.activation(out=mv[:, mt, :], in_=PS(bi), func=AF.Copy),
                       r=[("ps", bi)], w=[("mv", mt)])

                if bstop <= 1:
                    P.flush()
                    break
                hsl = {}
                for t in range(3):
                    hsl[t] = norm_a(x_d[b, t * 128:(t + 1) * 128, :], act_rstd=True)
                for t in range(16):
                    if t + 3 < 16:
                        hsl[t + 3] = norm_a(x_d[b, (t + 3) * 128:(t + 4) * 128, :], act_rstd=True)
                    norm_b(hsl[t], PC_NORMW, hT[:, :, t * 128:(t + 1) * 128], [("hT", t // 4, t % 4)])
                hT_all = [("hT", a, c) for a in range(4) for c in range(4)]
                op("pool", lambda e: e.memset(vg[:, :, :, 64:65], 1.0), w=["vg_ones"])

                def tok_view(ap3, g, blk512):
                    if g == 0:
                        return ap3[:, blk512 * 512:(blk512 + 1) * 512]
                    if g == 1:
                        return ap3[:, blk512:2048:4]
                    return ap3.rearrange("p (m r) -> p r m", r=16)[:, 4 * blk512:4 * blk512 + 4, :]

                def tok_view128(ap3, g, qb):
                    d = (1, 4, 16)[g]
                    nb = 16 // d
                    r, n = qb // nb, qb % nb
                    s0 = r + d * 128 * n
                    return ap3[:, s0:s0 + d * 127 + 1:d] if d > 1 else ap3[:, s0:s0 + 128]

                if bstop <= 2:
                    P.flush()
                    break
                if split_b:
                    P.flush()
                hblk = [xb[:].rearrange("p a c -> p (a c)").bitcast(BF16).rearrange("p (k t) -> p k t", k=8),
                        xn[:].rearrange("p a c -> p (a c)").rearrange("p (k t) -> p k t", k=8)]
                hbres = [[("xb", 0), ("xb", 1)], [("xn", 0), ("xn", 1), ("xn", 2), ("xn", 3)]]
                slotB = {}
                for g in range(3):
                    if bstop <= 3 + 2 * g:
                        break
                    d = (1, 4, 16)[g]
                    nb = 16 // d
                    wqk, wqkr = wget(("B0", "B2", "B3")[g])
                    slotB[g] = (wqk, wqkr)
                    if g == 0:
                        wv01, wv01r = wget("B1")
                    if g == 2:
                        wv2, wv2r = wget("B4")
                    if g < 2:
                        wvv, wvvr, vcol = wv01, wv01r, 256 * g
                    else:
                        wvv, wvvr, vcol = wv2, wv2r, 256
                    for blk in range(4):
                        if g > 0:
                            hb = hblk[blk % 2]
                            hbr = hbres[blk % 2]
                            if g == 1:
                                src_v = hT[:, :, blk:2048:4]
                                dst_v = hb
                            else:
                                src_v = hT[:].rearrange("p k (m r) -> p k r m", r=16)[:, :, 4 * blk:4 * blk + 4, :]
                                dst_v = hb.rearrange("p k (r m) -> p k r m", r=4)
                            op("pool", lambda e, src_v=src_v, dst_v=dst_v: e.tensor_copy(out=dst_v, in_=src_v),
                               r=hT_all, w=hbr)
                        for qk in range(2):
                            for pr in range(2):
                                bi = bank()
                                c0 = qk * 256 + pr * 128
                                hdeps = [("hT", blk, c_) for c_ in range(4)] if g == 0 else hbr
                                mm_acc(PS(bi), [(wqk[:, k, c0:c0 + 128], (tok_view(hT[:, k, :], g, blk) if g == 0 else hb[:, k, :]))
                                                for k in range(8)], bi, [wqkr] + hdeps)
                                ti = tmpi()
                                dst = (qg, kg)[qk]
                                if qk == 0:
                                    sc, bs_, wc = 1.0, 64 * EPS, PC_BQW
                                else:
                                    sc, bs_, wc = 1.0 / 64, EPS, PC_BKW
                                if ptail[0] is not None:
                                    ptail[0]()
                                ptail[0] = feat_norm(bi, 512, blk64[:], sc, bs_, wc, dst[:, pr, blk * 512:(blk + 1) * 512],
                                                     [(("qg", "kg")[qk], pr, blk)], tsq[:, ti, :], tsr[:, ti, :], "t%d" % ti,
                                                     split=True)
                        for half in range(2):
                            bi = bank()
                            for q2 in range(2):
                                qb = blk * 4 + half * 2 + q2
                                mm_acc(PS(bi)[:, q2 * 256:(q2 + 1) * 256],
                                       [((tok_view128(hT[:, k, :], g, qb) if g == 0 else hb[:, k, (qb % 4) * 128:(qb % 4 + 1) * 128]),
                                         wvv[:, k, vcol:vcol + 256]) for k in range(8)],
                                       bi, [wvvr] + ([("hT", qb // 4, qb % 4)] if g == 0 else hbr))
                                if ptail[0] is not None:
                                    ptail[0]()
                                    ptail[0] = None
                            qb0 = blk * 4 + half * 2
                            eng = "act" if half == 0 else "dve"
                            src = PS(bi).rearrange("p (q h c) -> p q h c", q=2, h=4)
                            dstv = vg[:, qb0:qb0 + 2, :, 0:64]
                            if eng == "act":
                                op("act", lambda e, src=src, dstv=dstv: e.activation(out=dstv, in_=src, func=AF.Copy),
                                   r=[("ps", bi)], w=[("vg", qb0), ("vg", qb0 + 1)])
                            else:
                                op("dve", lambda e, src=src, dstv=dstv: e.tensor_copy(out=dstv, in_=src),
                                   r=[("ps", bi)], w=[("vg", qb0), ("vg", qb0 + 1)])
                    if bstop <= 4 + 2 * g:
                        break
                    eidx = {0: (0, 1), 1: (2, 3), 2: (None, 4)}[g]
                    def q_copy(qc):
                        qs_ = qc % 4
                        op("dve", lambda e, qs_=qs_, qc=qc: e.tensor_copy(
                            out=qm[0:64, qs_, 0:4:2, :], in_=qg[0:64, :, qc * 128:(qc + 1) * 128]),
                           r=[("qg", 0, qc // 4), ("qg", 1, qc // 4)], w=[("qm", qs_)])
                        op("pool", lambda e, qs_=qs_, qc=qc: e.tensor_copy(
                            out=qm[64:128, qs_, 1:4:2, :], in_=qg[64:128, :, qc * 128:(qc + 1) * 128]),
                           r=[("qg", 0, qc // 4), ("qg", 1, qc // 4)], w=[("qm", qs_)])

                    def att_front(qb):
                        r_, n_ = qb // nb, qb % nb
                        blks = []
                        if n_ > 0:
                            blks.append((qb - 1, eidx[0]))
                        blks.append((qb, eidx[1]))
                        pts = []
                        qs = qb % 4
                        for qc in ([0, 1, 2] if qb == 0 else ([qb + 2] if qb + 2 < 16 else [])):
                            q_copy(qc)
                        for (kb, ei) in blks:
                            bi = bank()
                            for h in range(4):
                                pr = h // 2
                                op("pe", lambda e, bi=bi, h=h, pr=pr, kb=kb, qs=qs: e.matmul(
                                    PS(bi)[:, h * 128:(h + 1) * 128],
                                    lhsT=kg[:, pr, kb * 128:(kb + 1) * 128],
                                    rhs=qm[:, qs, h, :], start=True, stop=True),
                                   r=[("kg", pr, kb // 4), ("qm", qs)], w=[("ps", bi)])
                            ti = tmpi()
                            op("act", lambda e, bi=bi, ti=ti: e.activation(out=ex[:, ti, :], in_=PS(bi), func=AF.Exp),
                               r=[("ps", bi)], w=[("ex", ti)])
                            pctr[0] += 1
                            pi = pctr[0] % 4
                            op("dve",
                               lambda e, ti=ti, pi=pi, ei=ei: e.tensor_tensor(out=pTt[:, pi, :], in0=ex[:, ti, :],
                                                                             in1=E[:, ei, :], op=ALU.mult),
                               r=[("ex", ti), ("E", ei)], w=[("pT", pi)])
                            pts.append((kb, pi))
                        return pts

                    def att_back(qb, pts):
                        r_, n_ = qb // nb, qb % nb
                        bi = bank()
                        for h in range(4):
                            for i, (kb, pi) in enumerate(pts):
                                op("pe", lambda e, bi=bi, h=h, kb=kb, pi=pi, i=i, n=len(pts): e.matmul(
                                    PS(bi)[0:65, h * 128:(h + 1) * 128], lhsT=vg[:, kb, h, :],
                                    rhs=pTt[:, pi, h * 128:(h + 1) * 128], start=(i == 0), stop=(i == n - 1)),
                                   r=[("vg", kb), "vg_ones", ("pT", pi)], w=[("ps", bi)])
                        s0 = r_ + d * 128 * n_
                        accv = acc[0:65, :, s0:s0 + d * 127 + 1:d] if d > 1 else acc[0:65, :, s0:s0 + 128]
                        pvv = PS(bi)[0:65, :].rearrange("p (h q) -> p h q", h=4)
                        if g == 0:
                            op("act", lambda e, accv=accv, pvv=pvv: e.activation(out=accv, in_=pvv, func=AF.Copy),
                               r=[("ps", bi)], w=["acc"])
                        else:
                            op("dve", lambda e, accv=accv, pvv=pvv: e.tensor_tensor(out=accv, in0=accv, in1=pvv, op=ALU.add),
                               r=[("ps", bi), "acc"], w=["acc"])

                    pend = None
                    for qb in range(16):
                        pts_ = att_front(qb)
                        if pend is not None:
                            att_back(*pend)
                        pend = (qb, pts_)
                    att_back(*pend)
                    if split_b and g < 2:
                        P.flush()
                if bstop <= 9:
                    P.flush()
                    break
                if split_b:
                    P.flush()
                bhl = pTt
                for blk in range(4):
                    for h in range(4):
                        bi = bank()
                        mm_acc(PS(bi), [(wv2[:, k, 64 * h:64 * h + 128], hT[:, k, blk * 512:(blk + 1) * 512])
                                        for k in range(8)], bi, [wv2r] + [("hT", blk, c_) for c_ in range(4)])
                        op("act", lambda e, bi=bi, h=h, blk=blk: e.activation(out=ybT[0:64, h, blk * 512:(blk + 1) * 512],
                                                                             in_=PS(bi)[0:64, :], func=AF.Silu),
                           r=[("ps", bi)], w=[("ybT", blk)])
                def fin_front(it, blk, h):
                    s_ = it % 2
                    asl = acc[0:65, h, blk * 512:(blk + 1) * 512]
                    op("act", lambda e, asl=asl, s_=s_: e.activation(out=bhl[0:65, 2 * s_, :], in_=asl, func=AF.Copy),
                       r=["acc"], w=[("pT", 2 * s_)])
                    op("dve", lambda e, asl=asl, s_=s_: e.tensor_tensor(out=bhl[0:65, 2 * s_ + 1, :], in0=asl, in1=bhl[0:65, 2 * s_, :],
                                                                      op=ALU.subtract),
                       r=["acc", ("pT", 2 * s_)], w=[("pT", 2 * s_ + 1)])
                    b2 = bank()
                    for i_ in range(2):
                        op("pe", lambda e, b2=b2, i_=i_, s_=s_: e.matmul(PS(b2), lhsT=sel64[:, :], rhs=bhl[:, 2 * s_ + i_, :],
                                                                        start=(i_ == 0), stop=(i_ == 1)),
                           r=[("pT", 2 * s_), ("pT", 2 * s_ + 1), "sel64"], w=[("ps", b2)])
                    return b2

                def fin_back(it, blk, h, b2):
                    ti = it % 2
                    op("act", lambda e, b2=b2, ti=ti: e.activation(out=bt1[0:64, ti, :], in_=PS(b2)[0:64, :], func=AF.Ln),
                       r=[("ps", b2)], w=["t%dsr" % ti])
                    op("act", lambda e, ti=ti: e.activation(out=bt1[0:64, ti, :], in_=bt1[0:64, ti, :], func=AF.Exp, scale=-1.0),
                       r=["t%dsr" % ti], w=["t%dsr" % ti])
                    op("dve", lambda e, ti=ti, h=h, blk=blk: e.tensor_tensor(
                        out=bt1[0:64, ti, :], in0=acc[0:64, h, blk * 512:(blk + 1) * 512], in1=bt1[0:64, ti, :], op=ALU.mult),
                       r=["acc", "t%dsr" % ti], w=["t%dsr" % ti])
                    op("pool", lambda e, ti=ti, h=h, blk=blk: e.tensor_tensor(
                        out=ybT[0:64, h, blk * 512:(blk + 1) * 512], in0=bt1[0:64, ti, :],
                        in1=ybT[0:64, h, blk * 512:(blk + 1) * 512], op=ALU.mult),
                       r=["t%dsr" % ti, ("ybT", blk)], w=[("ybT", blk)])

                fpend = None
                for it, (blk, h) in enumerate([(blk_, h_) for blk_ in range(4) for h_ in range(4)]):
                    b2_ = fin_front(it, blk, h)
                    if fpend is not None:
                        fin_back(*fpend)
                    fpend = (it, blk, h, b2_)
                fin_back(*fpend)
                if debug and b == 0:
                    dbg_store("ybT", ybT[0:64].rearrange("p h t -> p (h t)"), [("ybT", i) for i in range(4)], 8192, 64)
                P.flush()

            if stop <= 2 * b + 1:
                break
            with contextlib.ExitStack() as phs:
                def pt(name, shape, dt):
                    return phs.enter_context(nc.sbuf_tensor(un(name), shape, dt))
                hTs2 = pt("hTs", [128, 2, 8, 512], BF16)
                vv = pt("vv", [128, 4, 832], BF16)
                op("pool", lambda e: e.memset(vv[:], 0.0), w=[("vv", j_) for j_ in range(4)])
                gv = pt("gv", [128, 768], F32)
                bnst = pt("bnst", [128, 2, 6], F32)
                gu = pt("gu", [128, 2, 512], BF16)
                szt = pt("sz", [128, 2, 512], BF16)
                uz = pt("uz", [128, 2, 512], BF16)
                tb = pt("tb", [128, 2, 512], F32)
                yaT = pt("yaT", [128, 8, 512], BF16)
                op("pool", lambda e: e.memset(yaT[:], 0.0), w=[("yaT", j_) for j_ in range(8)])
                tsq2 = pt("tsq2", [128, 2, 512], BF16)
                tsr2 = pt("tsr2", [128, 2, 512], F32)
                mqn = pt("mqn", [128, 4, 512], BF16)
                szm = pt("szm", [128, 4, 512], BF16)
                pmT = pt("pm", [128, 4, 512], BF16)
                rl = pt("rl", [128, 2, 512], F32)
                ymT = pt("ymT", [128, 4, 512], BF16)
                sg = pt("sg", [128, 2, 512], F32)
                mg = pt("mg", [128, 8, 512], F32)
                mgb = pt("mgb", [128, 8, 512], BF16)
                res = pt("res", [128, 2, 1024], F32)
                tc2 = [0]

                def t2():
                    tc2[0] += 1
                    return tc2[0] % 2

                hslots = {}

                def emit_hT_a(st_):
                    hslots[st_] = [norm_a(x_d[b, st_ * 512 + j_ * 128:st_ * 512 + (j_ + 1) * 128, :]) for j_ in range(4)]

                def emit_hT_b(st_):
                    for j_ in range(4):
                        norm_b(hslots[st_][j_], PC_NORMW, hTs2[:, st_ % 2, :, j_ * 128:(j_ + 1) * 128], [("hTs", st_ % 2, j_)])

                emit_hT_a(0)
                emit_hT_b(0)
                for st in range(4):
                    T0 = st * 512
                    hTs = hTs2[:, st % 2]
                    hTs_all = [("hTs", st % 2, j) for j in range(4)]
                    if debug and b == 0 and st == 0:
                        dbg_store("hT", hTs.rearrange("p k t -> p (k t)"), hTs_all, 4096)
                    wV0, wV0r = wget("V0")
                    wV1, wV1r = wget("V1")
                    for j in range(4):
                        for half in range(2):
                            wv_, wvr_ = (wV0, wV0r) if half == 0 else (wV1, wV1r)
                            bi = bank()
                            mm_acc(PS(bi)[:, 0:384], [(hTs[:, k, j * 128:(j + 1) * 128], wv_[:, k, 0:384]) for k in range(8)],
                                   bi, [wvr_, ("hTs", st % 2, j)])
                            op("act", lambda e, bi=bi, half=half: e.activation(out=gv[:, half * 384:(half + 1) * 384],
                                                                               in_=PS(bi)[:, 0:384], func=AF.Gelu),
                               r=[("ps", bi)], w=[("gv", half)])
                            op("dve", lambda e, half=half: e.bn_stats(out=bnst[:, half, :], in_=gv[:, half * 384:(half + 1) * 384]),
                               r=[("gv", half)], w=[("bnst", half)])
                        mvv = stat[:, 8:10]
                        rs = stat[:, 10:11]
                        op("dve", lambda e: e.bn_aggr(out=mvv, in_=bnst[:].rearrange("p a b -> p (a b)")),
                           r=[("bnst", 0), ("bnst", 1)], w=["mvv"])
                        op("dve", lambda e: e.tensor_scalar(out=rs, in0=stat[:, 9:10], scalar1=EPS, scalar2=None, op0=ALU.add),
                           r=["mvv"], w=["rs"])
                        op("pool", lambda e: e.tensor_tensor(out=rs, in0=rs, in1=params[:, PC_NEGHALF:PC_NEGHALF + 1], op=ALU.pow),
                           r=["rs", "params"], w=["rs"])
                        op("dve", lambda e, j=j: e.tensor_scalar(out=vv[:, j, 0:768], in0=gv[:], scalar1=stat[:, 8:9], scalar2=rs,
                                                            op0=ALU.subtract, op1=ALU.mult),
                           r=[("gv", 0), ("gv", 1), "mvv", "rs"], w=[("vv", j)])
                    wU = {}
                    for jc in range(8):
                        g_, part = jc // 2, jc % 2
                        cw = 128 if part == 0 else 64
                        co = 192 * g_ + 128 * part
                        if jc == 0:
                            wU[0] = wget("U0")
                            wU[1] = wget("Z0")
                        if jc == 5:
                            wU[0] = wget("U1")
                            wU[1] = wget("Z1")
                        c0 = co if co < 512 else co - 512
                        (wu, wur), (wz, wzr) = wU[0], wU[1]
                        bu = bank()
                        mm_acc(PS(bu), [(wu[:, k, c0:c0 + 128], hTs[:, k, :]) for k in range(8)], bu, [wur] + hTs_all)
                        bz = bank()
                        mm_acc(PS(bz), [(wz[:, k, c0:c0 + 128], hTs[:, k, :]) for k in range(8)], bz, [wzr] + hTs_all)
                        ti = t2()
                        op("act", lambda e, bu=bu, ti=ti, cw=cw: e.activation(out=gu[0:cw, ti, :], in_=PS(bu)[0:cw, :], func=AF.Gelu),
                           r=[("ps", bu)], w=[("gu", ti)])
                        op("act", lambda e, bz=bz, ti=ti, cw=cw: e.activation(out=szt[0:cw, ti, :], in_=PS(bz)[0:cw, :], func=AF.Silu),
                           r=[("ps", bz)], w=[("sz", ti)])
                        op("pool", lambda e, ti=ti, cw=cw: e.tensor_tensor(out=uz[0:cw, ti, :], in0=gu[0:cw, ti, :],
                                                                            in1=szt[0:cw, ti, :], op=ALU.mult),
                           r=[("gu", ti), ("sz", ti)], w=[("uz", ti)])
                        bm = bank()
                        for tcn in range(4):
                            op("pe", lambda e, bm=bm, tcn=tcn, co=co, cw=cw, g_=g_: e.matmul(
                                PS(bm)[:, tcn * 128:(tcn + 1) * 128], lhsT=vv[:, tcn, co:co + 128], rhs=wsT[:, g_, :],
                                start=True, stop=True), r=[("vv", tcn), "wsT"], w=[("ps", bm)])
                        op("dve", lambda e, bm=bm, ti=ti, cw=cw, jc=jc: e.scalar_tensor_tensor(
                            out=tb[0:cw, ti, :].rearrange("p (c t) -> p c t", c=4),
                            in0=PS(bm)[0:cw, :].rearrange("p (c t) -> p c t", c=4),
                            scalar=params[0:cw, PC_LNWC + jc:PC_LNWC + jc + 1],
                            in1=Cc[0:cw, jc, :].unsqueeze(1).broadcast_to([cw, 4, 128]),
                            op0=ALU.mult, op1=ALU.add), r=[("ps", bm), "params", "Cc"], w=[("tb", ti)])
                        op("dve", lambda e, ti=ti, cw=cw, jc=jc: e.tensor_tensor(out=yaT[0:cw, jc, :], in0=tb[0:cw, ti, :],
                                                                               in1=uz[0:cw, ti, :], op=ALU.mult),
                           r=[("tb", ti), ("uz", ti)], w=[("yaT", jc)])
                    if debug and b == 0 and st == 0:
                        dbg_store("yaT", yaT[:].rearrange("p k t -> p (k t)"), [("yaT", i) for i in range(8)], 4096)

                    def merge_branch(br, gnames, pfun, first, last):
                        wg = [wget(gnames[0]), wget(gnames[1])]
                        pinfo = pfun()
                        for c in range(8):
                            half, cc = c // 4, c % 4
                            bg = bank()
                            mm_acc(PS(bg), [(wg[half][0][:, k, cc * 128:(cc + 1) * 128], hTs[:, k, :]) for k in range(8)],
                                   bg, [wg[half][1]] + hTs_all)
                            bp = bank()
                            pairs, preads = pinfo(half, cc)
                            mm_acc(PS(bp), pairs, bp, preads)
                            ti = t2()
                            op("act", lambda e, bg=bg, ti=ti, c=c: e.activation(
                                out=sg[:, ti, :], in_=PS(bg), func=AF.Sigmoid,
                                bias=params[:, PC_GATEB + 8 * br + c:PC_GATEB + 8 * br + c + 1], scale=1.0),
                               r=[("ps", bg), "params"], w=[("sg", ti)])
                            if first:
                                op("dve", lambda e, bp=bp, ti=ti, c=c: e.tensor_tensor(out=mg[:, c, :], in0=sg[:, ti, :],
                                                                                     in1=PS(bp), op=ALU.mult),
                                   r=[("sg", ti), ("ps", bp)], w=[("mg", c)])
                            else:
                                op("dve", lambda e, bp=bp, ti=ti: e.tensor_tensor(out=sg[:, ti, :], in0=sg[:, ti, :],
                                                                                in1=PS(bp), op=ALU.mult),
                                   r=[("sg", ti), ("ps", bp)], w=[("sg", ti)])
                                if last:
                                    op("pool", lambda e, ti=ti, c=c: e.tensor_tensor(out=mgb[:, c, :], in0=mg[:, c, :],
                                                                                     in1=sg[:, ti, :], op=ALU.add),
                                       r=[("sg", ti), ("mg", c)], w=[("mgb", c)])
                                else:
                                    op("pool", lambda e, ti=ti, c=c: e.tensor_tensor(out=mg[:, c, :], in0=mg[:, c, :],
                                                                                     in1=sg[:, ti, :], op=ALU.add),
                                       r=[("sg", ti), ("mg", c)], w=[("mg", c)])

                    def pfun_a():
                        pa = [wget("PA0"), wget("PA1")]

                        def info(half, cc):
                            pairs = []
                            for jc in range(8):
                                pairs.append((pa[half][0][:, jc, cc * 128:(cc + 1) * 128], yaT[:, jc, :]))
                            return pairs, [pa[half][1]] + [("yaT", i) for i in range(8)]
                        return info

                    if st < 3:
                        emit_hT_a(st + 1)
                    merge_branch(0, ("G0", "G1"), pfun_a, True, False)

                    wMQ, wMQr = wget("MQ")
                    wMZ, wMZr = wget("MZ")
                    for h in range(4):
                        bz = bank()
                        mm_acc(PS(bz), [(wMZ[:, k, h * 128:(h + 1) * 128], hTs[:, k, :]) for k in range(8)], bz, [wMZr] + hTs_all)
                        op("act", lambda e, bz=bz, h=h: e.activation(out=szm[:, h, :], in_=PS(bz), func=AF.Silu),
                           r=[("ps", bz)], w=[("szm", h)])

                    def m_front(h):
                        bq = bank()
                        mm_acc(PS(bq), [(wMQ[:, k, h * 128:(h + 1) * 128], hTs[:, k, :]) for k in range(8)], bq, [wMQr] + hTs_all)
                        ti = t2()
                        feat_norm(bq, 512, ones_bf[:], 1.0, 128 * EPS, PC_MQW, mqn[:, h, :], [("mqn", h)],
                                  tsq2[:, ti, :], tsr2[:, ti, :], "u%d" % ti)

                    def m_back(h):
                        pis = []
                        for mc in range(2):
                            bs = bank()
                            op("pe", lambda e, bs=bs, h=h, mc=mc: e.matmul(PS(bs), lhsT=mkT[:, h, mc * 128:(mc + 1) * 128],
                                                                           rhs=mqn[:, h, :], start=True, stop=True),
                               r=[("mkT", h), ("mqn", h)], w=[("ps", bs)])
                            pi = (2 * h + mc) % 4
                            op("act", lambda e, bs=bs, pi=pi: e.activation(out=pmT[:, pi, :], in_=PS(bs), func=AF.Exp),
                               r=[("ps", bs)], w=[("pm", pi)])
                            pis.append(pi)
                        by = bank()
                        mm_acc(PS(by), [(mv[:, mc, h * 128:(h + 1) * 128], pmT[:, pis[mc], :]) for mc in range(2)], by,
                               [("mv", 0), ("mv", 1), ("pm", pis[0]), ("pm", pis[1])])
                        bl = bank()
                        mm_acc(PS(bl), [(ones_bf[:], pmT[:, pis[mc], :]) for mc in range(2)], bl,
                               ["ones_bf", ("pm", pis[0]), ("pm", pis[1])])
                        ti = h % 2
                        op("act", lambda e, bl=bl, ti=ti: e.activation(out=rl[:, ti, :], in_=PS(bl), func=AF.Ln), r=[("ps", bl)], w=[("rl", ti)])
                        op("act", lambda e, ti=ti: e.activation(out=rl[:, ti, :], in_=rl[:, ti, :], func=AF.Exp, scale=-1.0),
                           r=[("rl", ti)], w=[("rl", ti)])
                        op("dve", lambda e, by=by, ti=ti: e.tensor_tensor(out=rl[:, ti, :], in0=rl[:, ti, :], in1=PS(by), op=ALU.mult),
                           r=[("ps", by), ("rl", ti)], w=[("rl", ti)])
                        op("pool", lambda e, ti=ti, h=h: e.tensor_tensor(out=ymT[:, h, :], in0=rl[:, ti, :], in1=szm[:, h, :], op=ALU.mult),
                           r=[("rl", ti), ("szm", h)], w=[("ymT", h)])

                    m_front(0)
                    m_front(1)
                    m_back(0)
                    m_front(2)
                    m_back(1)
                    m_front(3)
                    m_back(2)
                    m_back(3)
                    if debug and b == 0 and st == 0:
                        dbg_store("ymT", ymT[:].rearrange("p k t -> p (k t)"), [("ymT", i) for i in range(4)], 2048)

                    def pfun_b():
                        pb = wget("PB")

                        def info(half, cc):
                            pairs = [(pb[0][:, 4 * half + h, cc * 128:(cc + 1) * 128], ybT[:, h, T0:T0 + 512]) for h in range(4)]
                            return pairs, [pb[1], ("ybT", st)]
                        return info

                    def pfun_m():
                        pmw = wget("PM")

                        def info(half, cc):
                            pairs = [(pmw[0][:, 4 * half + h, cc * 128:(cc + 1) * 128], ymT[:, h, :]) for h in range(4)]
                            return pairs, [pmw[1]] + [("ymT", i) for i in range(4)]
                        return info

                    if st < 3:
                        emit_hT_b(st + 1)
                    merge_branch(1, ("G2", "G3"), pfun_b, False, False)
                    merge_branch(2, ("G4", "G5"), pfun_m, False, True)
                    if debug and b == 0 and st == 0:
                        dbg_store("mg", mgb[:].rearrange("p k t -> p (k t)"), [("mgb", i) for i in range(8)], 4096)

                    wo = [wget("WO0"), wget("WO1")]
                    for j in range(4):
                        rj = j % 2
                        xsrc = x_d[b, T0 + j * 128:T0 + (j + 1) * 128, :]
                        odst = out_d[b, T0 + j * 128:T0 + (j + 1) * 128, :]
                        op("sp", lambda e, rj=rj, xsrc=xsrc: e.dma_start(out=res[:, rj, :], in_=xsrc),
                           w=[("res", rj)], dma="res%d" % rj)
                        for half in range(2):
                            bo = bank()
                            mm_acc(PS(bo), [(mgb[:, c, j * 128:(j + 1) * 128], wo[half][0][:, c, :]) for c in range(8)], bo,
                                   [wo[half][1]] + [("mgb", c) for c in range(8)])
                            op("dve", lambda e, bo=bo, rj=rj, half=half: e.tensor_tensor(
                                out=res[:, rj, half * 512:(half + 1) * 512], in0=res[:, rj, half * 512:(half + 1) * 512],
                                in1=PS(bo), op=ALU.add), r=[("ps", bo), ("res", rj)], w=[("res", rj)])
                        op("sp", lambda e, rj=rj, odst=odst: e.dma_start(out=odst, in_=res[:, rj, :]),
                           r=[("res", rj)], w=[], dma="res%d" % rj)
                op("sp", lambda e: e.nop(), r=[("dmatag", "res0"), ("dmatag", "res1"), ("dmatag", "dbg")], w=[])
                P.flush()
    return nc


def _prep_shared(inp):
    f = np.float32
    w_in = np.asarray(inp["w_in"], f)
    offs = np.cumsum([0, 768, 768, 768, 768, 768, 768, 256, 512, 512, 3072])
    a_u, a_v, a_z, b_q, b_k, b_v, b_z, m_q, m_z, g = [w_in[:, offs[i]:offs[i + 1]] for i in range(10)]
    img = np.zeros((NIMG, 1024, 512), f)

    def put(name, *cols):
        c = 0
        for a in cols:
            img[IMG[name], :, c:c + a.shape[1]] = a
            c += a.shape[1]

    for gi, nm in enumerate(("B0", "B2", "B3")):
        put(nm, b_q[:, 256 * gi:256 * gi + 256], b_k[:, 256 * gi:256 * gi + 256])
    put("B1", b_v[:, 0:256], b_v[:, 256:512])
    put("B4", b_z, b_v[:, 512:768])
    mwkv = np.asarray(inp["m_w_kv"], f)
    put("M0a", mwkv[:, 0:512])
    put("M0b", mwkv[:, 512:1024])
    put("V0", a_v[:, 0:384])
    put("V1", a_v[:, 384:768])
    put("U0", a_u[:, 0:512])
    put("U1", a_u[:, 512:768])
    put("Z0", a_z[:, 0:512])
    put("Z1", a_z[:, 512:768])
    put("MQ", m_q)
    put("MZ", m_z)
    for i in range(6):
        put("G%d" % i, g[:, 512 * i:512 * (i + 1)])
    pa = np.asarray(inp["proj_a"], f)
    for jc in range(8):
        g_, part = jc // 2, jc % 2
        cw = 128 if part == 0 else 64
        r0 = 192 * g_ + 128 * part
        for half in range(2):
            img[IMG["PA%d" % half], jc * 128:jc * 128 + cw, :] = pa[r0:r0 + cw, half * 512:(half + 1) * 512]
    pb = np.asarray(inp["proj_b"], f)
    pm = np.asarray(inp["proj_m"], f)
    for half in range(2):
        for h in range(4):
            img[IMG["PB"], (4 * half + h) * 128:(4 * half + h) * 128 + 64, :] = pb[64 * h:64 * h + 64, half * 512:(half + 1) * 512]
            img[IMG["PM"], (4 * half + h) * 128:(4 * half + h + 1) * 128, :] = pm[128 * h:128 * h + 128, half * 512:(half + 1) * 512]
    wo = np.asarray(inp["w_out"], f)
    put("WO0", wo[:, 0:512])
    put("WO1", wo[:, 512:1024])

    params = np.zeros((128, NP), f)
    params[:, PC_NORMW:PC_NORMW + 8] = np.asarray(inp["norm_w"], f).reshape(8, 128).T
    params[:, PC_MEMW:PC_MEMW + 8] = np.asarray(inp["mem_norm_w"], f).reshape(8, 128).T
    gb = np.asarray(inp["gate_b"], f)
    for br in range(3):
        params[:, PC_GATEB + 8 * br:PC_GATEB + 8 * br + 8] = gb[br].reshape(8, 128).T
    params[:, PC_BQW] = np.tile(np.asarray(inp["b_q_norm_w"], f), 2)
    params[:, PC_BKW] = np.tile(np.asarray(inp["b_k_norm_w"], f), 2)
    params[:, PC_MQW] = np.asarray(inp["m_q_norm_w"], f)
    params[:, PC_MKW] = np.asarray(inp["m_k_norm_w"], f)
    params[:, PC_NEGHALF] = -0.5
    params[:, PC_BS:PC_BS + 512] = np.asarray(inp["a_spatial_b"], f).reshape(1, 512)
    lnw = np.asarray(inp["a_v_norm_w"], f)
    lnb = np.asarray(inp["a_v_norm_b"], f)
    for jc in range(8):
        cw = 128 if jc % 2 == 0 else 64
        co = 192 * (jc // 2) + 128 * (jc % 2)
        params[0:cw, PC_LNWC + jc] = lnw[co:co + cw]
        params[0:cw, PC_LNBC + jc] = lnb[co:co + cw]

    wst = np.ascontiguousarray(np.transpose(np.asarray(inp["a_spatial_w"], f), (2, 0, 1))).reshape(128, 512)

    cst = np.zeros((128, 512), f)
    cst[:, 0:128] = np.eye(128, dtype=f)
    blk = np.zeros((128, 128), f)
    blk[0:64, 0:64] = 1
    blk[64:, 64:] = 1
    cst[:, 128:256] = blk
    kj = np.arange(128)[:, None]
    qi = np.arange(128)[None, :]
    cst[:, 256:384] = (kj >= qi).astype(f)
    cst[:, 384:512] = (kj <= qi).astype(f)

    rb = np.asarray(inp["rel_bias"], f)
    biasT = np.zeros((5, 128, 512), f)
    tiles = [(0, 0), (0, 1), (1, 0), (1, 1), (2, 1)]
    for idx, (gi, cur) in enumerate(tiles):
        d = (1, 4, 16)[gi]
        step = (qi - kj) if cur else (qi + 128 - kj)
        bk = _bucket(np.maximum(step, 0) * d)
        for h in range(4):
            biasT[idx, :, h * 128:(h + 1) * 128] = rb[bk, 4 * gi + h]
    return dict(wimg=img, params=params, wst=wst, cst=cst, biasT=biasT)


_CACHE = {}


def kernel(**inputs):
    inputs = dict(inputs)
    debug = bool(inputs.pop("_debug", False))
    _nc_override = inputs.pop("_ncores", None)
    shared = _prep_shared(inputs)
    x = np.ascontiguousarray(np.asarray(inputs["x"], np.float32))
    mem = np.ascontiguousarray(np.asarray(inputs["mem"], np.float32))
    stop = int(inputs.pop("_stop", 99))
    bstop = int(inputs.pop("_bstop", 99))
    key = ("nc", debug, stop, bstop)
    if key not in _CACHE:
        _CACHE[key] = build(debug, stop, bstop)
    nc = _CACHE[key]
    ncores = int(_nc_override) if _nc_override is not None else NCORES
    in_maps = []
    for c in range(ncores):
        m = dict(shared)
        m["x"] = x[c * BPC:(c + 1) * BPC]
        m["mem"] = mem[c * BPC:(c + 1) * BPC]
        in_maps.append(m)
    r = run_bass_kernel_spmd(nc, in_maps, core_ids=list(range(ncores)))
    out = np.concatenate([np.asarray(rr["out"]) for rr in r.results], axis=0).astype(np.float32)
    if debug:
        return out, r.results
    return out
```
